# Optimizing a Trainium2 kernel written in Bass

```python
import math
import jax, jax.numpy as jnp
from jax import lax
import numpy as np

D_MODEL = 1024
BATCH = 32
SEQ = 2048
DEPTH = 1
DEC_BATCH = 8
DEC_SEQ = 2048
PAST_LEN = 128

HEAD_DIM = 64
N_HEADS_A = 8
N_KV_A = 2
WIN_A = 128
DILATED_GROUPS = ((128, 1), (512, 4), (2048, 16))
HB_PER_GROUP = 4
N_HEADS_B = HB_PER_GROUP * len(DILATED_GROUPS)
N_HEADS_M = 4
HEAD_DIM_M = 128
N_MEM = 256
D_FF = 4 * D_MODEL
NUM_BUCKETS = 32
MAX_DIST = 1024
N_BIAS_HEADS = N_HEADS_A + N_HEADS_B
EPS = 1e-6
NEG = -1e30

W_A = N_HEADS_A * HEAD_DIM
W_KV_A = N_KV_A * HEAD_DIM
W_B = N_HEADS_B * HEAD_DIM
W_B_OUT = HB_PER_GROUP * HEAD_DIM
W_M = N_HEADS_M * HEAD_DIM_M
IN_SPLITS = (W_A, W_KV_A, W_KV_A, W_B, W_B, W_B, W_M, D_MODEL, D_MODEL, D_MODEL)
N_IN = sum(IN_SPLITS)

kernel_name = "hybrid_gated_local_dilated_memory_encoder"


def t5_bucket(rel):
    half = NUM_BUCKETS // 2
    ret = (rel > 0).astype(np.int32) * half
    n = np.abs(rel)
    max_exact = half // 2
    large = max_exact + (np.log(np.maximum(n, 1) / max_exact) / np.log(MAX_DIST / max_exact)
                         * (half - max_exact)).astype(np.int32)
    large = np.minimum(large, half - 1)
    return (ret + np.where(n < max_exact, n, large)).astype(np.int32)


def rms_norm(x, g):
    xf = x.astype(jnp.float32)
    y = xf * lax.rsqrt(jnp.mean(xf * xf, axis=-1, keepdims=True) + EPS)
    return (y * g.astype(jnp.float32)).astype(x.dtype)


def banded_attention(q, k, v, bias_off, half, sink):
    B_, L, H, Dh = q.shape
    G = k.shape[2]
    R = H // G
    W = half
    nb = -(-L // W)
    Lp = nb * W
    pad = Lp - L
    qb = jnp.pad(q, ((0, 0), (0, pad), (0, 0), (0, 0))).reshape(B_, nb, W, G, R, Dh)

    def key_blocks(t):
        tp = jnp.pad(t, ((0, 0), (W, pad + W), (0, 0), (0, 0))).reshape(B_, nb + 2, W, G, Dh)
        return jnp.concatenate([tp[:, :-2], tp[:, 1:-1], tp[:, 2:]], axis=2)

    kb = key_blocks(k)
    vb = key_blocks(v)
    valid = np.pad(np.ones(L, bool), (W, pad + W)).reshape(nb + 2, W)
    valid = np.concatenate([valid[:-2], valid[1:-1], valid[2:]], axis=1)
    off = np.arange(3 * W)[None, :] - W - np.arange(W)[:, None]
    mask = (np.abs(off) <= W)[None] & valid[:, None, :]
    bias = bias_off[:, np.clip(off + W, 0, 2 * W)].astype(jnp.float32).reshape(G, R, W, 3 * W)
    s = jnp.einsum('bnqgrd,bnkgd->bngrqk', qb, kb,
                   preferred_element_type=jnp.float32) * (Dh ** -0.5) + bias
    s = jnp.where(mask[None, :, None, None], s, NEG)
    m = jnp.max(s, axis=-1)
    if sink is not None:
        sk = sink.astype(jnp.float32).reshape(G, R)[None, None, :, :, None]
        m = jnp.maximum(m, sk)
    p = jnp.exp(s - m[..., None])
    den = jnp.sum(p, axis=-1)
    if sink is not None:
        den = den + jnp.exp(sk - m)
    out = jnp.einsum('bngrqk,bnkgd->bnqgrd', p, vb.astype(jnp.float32))
    out = out / jnp.moveaxis(den, -1, 2)[..., None]
    out = out.reshape(B_, Lp, H, Dh)[:, :L].astype(q.dtype)
    lse = jnp.moveaxis(m + jnp.log(den), -1, 2).reshape(B_, Lp, H)[:, :L]
    return out, lse


def dilated_attention(q, k, v, rel_bias):
    B_, S, _, Dh = q.shape
    outs, lses = [], []
    for g, (win, dil) in enumerate(DILATED_GROUPS):
        hs = slice(g * HB_PER_GROUP, (g + 1) * HB_PER_GROUP)
        half = win // (2 * dil)
        Ld = S // dil

        def strided(t):
            return (t[:, :, hs].reshape(B_, Ld, dil, HB_PER_GROUP, Dh)
                    .transpose(0, 2, 1, 3, 4).reshape(B_ * dil, Ld, HB_PER_GROUP, Dh))

        cols = N_HEADS_A + g * HB_PER_GROUP
        rel = dil * np.arange(-half, half + 1)
        bias_off = rel_bias[t5_bucket(rel)][:, cols:cols + HB_PER_GROUP].T
        o, l = banded_attention(strided(q), strided(k), strided(v), bias_off, half, None)
        o = o.reshape(B_, dil, Ld, HB_PER_GROUP, Dh).transpose(0, 2, 1, 3, 4).reshape(B_, S, HB_PER_GROUP, Dh)
        l = l.reshape(B_, dil, Ld, HB_PER_GROUP).transpose(0, 2, 1, 3).reshape(B_, S, HB_PER_GROUP)
        outs.append(o)
        lses.append(l)
    w = jax.nn.softmax(jnp.stack(lses), axis=0)
    out = jnp.sum(w[..., None] * jnp.stack(outs).astype(jnp.float32), axis=0)
    return out.astype(q.dtype)


def memory_attention(q, mk, mv):
    s = jnp.einsum('bshd,bmhd->bhsm', q, mk, preferred_element_type=jnp.float32) * (q.shape[-1] ** -0.5)
    p = jax.nn.softmax(s, axis=-1)
    return jnp.einsum('bhsm,bmhd->bshd', p, mv.astype(jnp.float32)).astype(q.dtype)


def encoder_layer(x, mem, rel_bias, norm1_g, w_in, mem_norm_g, w_mem_kv, sink_logit,
                  w_branch_a, w_branch_b, w_branch_m, w_out, norm2_g, w_up, w_down):
    B_, S, _ = x.shape
    h = rms_norm(x, norm1_g)
    proj = h @ w_in
    qa, ka, va, qb, kb, vb, qm, ga, gb, gm = jnp.split(
        proj, [int(c) for c in np.cumsum(IN_SPLITS)[:-1]], axis=-1)
    bias_a = rel_bias[t5_bucket(np.arange(-WIN_A, WIN_A + 1))][:, :N_HEADS_A].T
    oa, _ = banded_attention(qa.reshape(B_, S, N_HEADS_A, HEAD_DIM),
                             ka.reshape(B_, S, N_KV_A, HEAD_DIM),
                             va.reshape(B_, S, N_KV_A, HEAD_DIM), bias_a, WIN_A, sink_logit)
    ob = dilated_attention(qb.reshape(B_, S, N_HEADS_B, HEAD_DIM),
                           kb.reshape(B_, S, N_HEADS_B, HEAD_DIM),
                           vb.reshape(B_, S, N_HEADS_B, HEAD_DIM), rel_bias)
    mkv = rms_norm(mem, mem_norm_g) @ w_mem_kv
    mk, mv = jnp.split(mkv, 2, axis=-1)
    Mn = mem.shape[1]
    om = memory_attention(qm.reshape(B_, S, N_HEADS_M, HEAD_DIM_M),
                          mk.reshape(B_, Mn, N_HEADS_M, HEAD_DIM_M),
                          mv.reshape(B_, Mn, N_HEADS_M, HEAD_DIM_M))
    merged = (jax.nn.sigmoid(ga) * (oa.reshape(B_, S, W_A) @ w_branch_a)
              + jax.nn.sigmoid(gb) * (ob.reshape(B_, S, W_B_OUT) @ w_branch_b)
              + jax.nn.sigmoid(gm) * (om.reshape(B_, S, W_M) @ w_branch_m))
    x = x + merged @ w_out
    h2 = rms_norm(x, norm2_g)
    x = x + jnp.square(jax.nn.relu(h2 @ w_up)) @ w_down
    return x


def trunk(x, mem, rel_bias, norm1_g, w_in, mem_norm_g, w_mem_kv, sink_logit,
          w_branch_a, w_branch_b, w_branch_m, w_out, norm2_g, w_up, w_down, final_norm_g):
    for layer in range(DEPTH):
        x = encoder_layer(x, mem, rel_bias, norm1_g[layer], w_in[layer], mem_norm_g[layer],
                          w_mem_kv[layer], sink_logit[layer], w_branch_a[layer], w_branch_b[layer],
                          w_branch_m[layer], w_out[layer], norm2_g[layer], w_up[layer], w_down[layer])
    return rms_norm(x, final_norm_g)


def setup_inputs(seed: int = 0) -> dict:
    key = jax.random.key(seed)
    ks = jax.random.split(key, 20)
    f32 = jnp.float32

    def dense(k, shape):
        return jax.random.normal(k, shape, f32) * (shape[-2] ** -0.5)

    def gain(k, shape):
        return 1.0 + 0.1 * jax.random.normal(k, shape, f32)

    return {
        "x_prompt": jax.random.normal(ks[0], (BATCH, SEQ, D_MODEL), f32),
        "x_sample": jax.random.normal(ks[1], (DEC_BATCH, DEC_SEQ, D_MODEL), f32),
        "mem_prompt": jax.random.normal(ks[2], (BATCH, N_MEM, D_MODEL), f32),
        "mem_sample": jax.random.normal(ks[3], (DEC_BATCH, N_MEM, D_MODEL), f32),
        "rel_bias": 0.5 * jax.random.normal(ks[4], (NUM_BUCKETS, N_BIAS_HEADS), f32),
        "norm1_g": gain(ks[5], (DEPTH, D_MODEL)),
        "w_in": dense(ks[6], (DEPTH, D_MODEL, N_IN)),
        "mem_norm_g": gain(ks[7], (DEPTH, D_MODEL)),
        "w_mem_kv": dense(ks[8], (DEPTH, D_MODEL, 2 * W_M)),
        "sink_logit": 0.5 * jax.random.normal(ks[9], (DEPTH, N_HEADS_A), f32),
        "w_branch_a": dense(ks[10], (DEPTH, W_A, D_MODEL)),
        "w_branch_b": dense(ks[11], (DEPTH, W_B_OUT, D_MODEL)),
        "w_branch_m": dense(ks[12], (DEPTH, W_M, D_MODEL)),
        "w_out": dense(ks[13], (DEPTH, D_MODEL, D_MODEL)),
        "norm2_g": gain(ks[14], (DEPTH, D_MODEL)),
        "w_up": dense(ks[15], (DEPTH, D_MODEL, D_FF)),
        "w_down": dense(ks[16], (DEPTH, D_FF, D_MODEL)),
        "final_norm_g": gain(ks[17], (D_MODEL,)),
    }


def reference(x_prompt, x_sample, mem_prompt, mem_sample, rel_bias, norm1_g, w_in, mem_norm_g,
              w_mem_kv, sink_logit, w_branch_a, w_branch_b, w_branch_m, w_out, norm2_g, w_up,
              w_down, final_norm_g):
    y_prompt = trunk(x_prompt, mem_prompt, rel_bias, norm1_g, w_in, mem_norm_g, w_mem_kv, sink_logit,
                     w_branch_a, w_branch_b, w_branch_m, w_out, norm2_g, w_up, w_down, final_norm_g)
    y_sample = trunk(x_sample, mem_sample, rel_bias, norm1_g, w_in, mem_norm_g, w_mem_kv, sink_logit,
                     w_branch_a, w_branch_b, w_branch_m, w_out, norm2_g, w_up, w_down, final_norm_g)
    return (y_prompt, y_sample)
```

```python
import numpy as np
import concourse.bass as bass
import concourse.mybir as mybir
from concourse.bass_utils import run_bass_kernel_spmd

F32 = mybir.dt.float32
BF16 = mybir.dt.bfloat16
AF = mybir.ActivationFunctionType
ALU = mybir.AluOpType

D = 1024
S = 2048
NMEM = 256
DFF = 4096
NCORES = 8
SLOT = 4352
NSLOT = 4
TB = 512
NEGM = -30000.0

BLK_KV0, BLK_KV1, BLK_A0, BLK_A1 = 0, 1, 2, 3
BLK_B = 4
BLK_M0 = 10
BLK_T1 = 11
BLK_O = 19
BLK_U = 21
BLK_D = 29
NBLK = 37


def _t5_bucket(rel):
    half = 16
    ret = (rel > 0).astype(np.int32) * half
    n = np.abs(rel)
    max_exact = half // 2
    large = max_exact + (np.log(np.maximum(n, 1) / max_exact) / np.log(1024 / max_exact)
                         * (half - max_exact)).astype(np.int32)
    large = np.minimum(large, half - 1)
    return (ret + np.where(n < max_exact, n, large)).astype(np.int32)


def _w_in_perm():
    cols = []
    for i in range(4):
        cols += list(range(64 * i, 64 * i + 64)) + list(range(256 + 64 * i, 256 + 64 * i + 64))
    cols += list(range(512, 768))
    for g in range(3):
        for jp in range(2):
            h0 = 4 * g + 2 * jp
            for base in (768, 1536, 2304):
                cols += list(range(base + 64 * h0, base + 64 * h0 + 128))
    cols += list(range(3072, 3584))
    for fc in range(8):
        for base in (3584, 4608, 5632):
            cols += list(range(base + 128 * fc, base + 128 * fc + 128))
    assert len(cols) == 6656 and len(set(cols)) == 6656
    return np.array(cols)


def _bias_tables(rel_bias):
    kp = np.arange(128)[:, None]
    qp = np.arange(128)[None, :]
    tabA = np.full((128, 8, 3, 128), NEGM, np.float32)
    for b in range(3):
        off = (b - 1) * 128 + kp - qp
        valid = np.abs(off) <= 128
        bk = _t5_bucket(np.clip(off, -128, 128))
        for h in range(8):
            tabA[:, h, b, :] = np.where(valid, rel_bias[bk, h], NEGM)
    tabB = np.full((128, 3, 4, 3, 128), NEGM, np.float32)
    for g, dil in enumerate((1, 4, 16)):
        for b in range(3):
            off = (b - 1) * 128 + kp - qp
            valid = np.abs(off) <= 64
            bk = _t5_bucket(dil * np.clip(off, -64, 64))
            for j in range(4):
                tabB[:, g, j, b, :] = np.where(valid, rel_bias[bk, 8 + 4 * g + j], NEGM)
    return tabA.reshape(128, -1), tabB.reshape(128, -1)


class Buf:
    __slots__ = ("w", "r")

    def __init__(self):
        self.w = {}
        self.r = {}


class Sched:
    ENGS = ("pe", "act", "dve", "pool", "sp")

    def __init__(self):
        self.q = {e: [] for e in self.ENGS}
        self.cnt = {}
        self.seen = {e: {} for e in self.ENGS}
        self.own = {"pe": "s_pe", "act": "s_act", "dve": "s_dve", "pool": "s_pool"}
        for s in self.own.values():
            self.cnt[s] = 0

    def wait(self, eng, sem, val):
        if self.seen[eng].get(sem, 0) >= val:
            return
        self.seen[eng][sem] = val
        self.q[eng].append(("w", sem, val))

    def run(self, eng, fns, reads=(), writes=(), dsem=None, extra=()):
        if not isinstance(fns, (list, tuple)):
            fns = [fns]
        own = self.own.get(eng)
        deps = {}
        for b in reads:
            for s, v in b.w.items():
                deps[s] = max(deps.get(s, 0), v)
        for b in writes:
            for s, v in list(b.w.items()) + list(b.r.items()):
                deps[s] = max(deps.get(s, 0), v)
        for (s, v) in extra:
            deps[s] = max(deps.get(s, 0), v)
        for s, v in deps.items():
            if eng == "pe" and s == own:
                continue
            self.wait(eng, s, v)
        if dsem is not None:
            assert len(fns) == 1
            self.cnt[dsem] = self.cnt.get(dsem, 0) + 16
            ev = (dsem, self.cnt[dsem])
            self.q[eng].append(("o", fns[0], dsem, 16))
        else:
            for f in fns[:-1]:
                self.q[eng].append(("o", f, None, 0))
            self.cnt[own] += 1
            ev = (own, self.cnt[own])
            self.q[eng].append(("o", fns[-1], own, 1))
        for b in reads:
            b.r[ev[0]] = ev[1]
        for b in writes:
            b.w = {ev[0]: ev[1]}
            b.r = {}
        return ev

    def barrier(self, engs=("pe", "act", "dve", "pool", "sp")):
        snap = dict(self.cnt)
        for e in engs:
            for s, v in snap.items():
                if v > 0 and not s.startswith("d_c"):
                    self.wait(e, s, v)

    def replay(self, eng, e, sems):
        for it in self.q[eng]:
            if it[0] == "w":
                e.wait_ge(sems[it[1]], it[2])
            else:
                inst = it[1](e)
                if it[2] is not None:
                    inst.then_inc(sems[it[2]], it[3])


class _Stop(Exception):
    pass


STAGE = 99


def _ck(n):
    if STAGE == n:
        raise _Stop()


def build(NSEQ):
    nc = bass.Bass("TRN2", target_bir_lowering=False)

    def din(name, shape, dt=F32):
        return nc.dram_tensor(name, shape, dt, kind="ExternalInput").ap()

    x_d = din("x", [NSEQ, S, D])
    mem_d = din("mem", [NSEQ, NMEM, D])
    y_d = nc.dram_tensor("y", [NSEQ, S, D], F32, kind="ExternalOutput").ap()
    w_in_d = din("w_in_p", [D, 6656])
    w_kv_d = din("w_kv", [D, 1024])
    w_a_d = din("w_a", [512, D])
    w_b_d = din("w_b", [256, D])
    w_m_d = din("w_m", [512, D])
    w_o_d = din("w_o", [D, D])
    w_up_d = din("w_up", [D, DFF])
    w_dn_d = din("w_dn", [DFF, D])
    tabA_d = din("tabA", [128, 8 * 3 * 128])
    tabB_d = din("tabB", [128, 3 * 4 * 3 * 128])
    gv_d = din("gv", [128, 24])
    gf_d = din("gf", [1, D])
    sink_d = din("sink", [1, 8])
    ident_d = din("ident", [128, 128])
    wimg = nc.dram_tensor("wimg", [NBLK, 128, SLOT], BF16).ap()

    sc = Sched()
    sem_names = ["s_pe", "s_act", "s_dve", "s_pool", "d_setup", "d_c0", "d_c1", "d_c2"] + \
        ["d_w%d" % i for i in range(NSLOT)] + ["d_x%d" % i for i in range(3)] + ["d_st%d" % i for i in range(4)]

    from contextlib import ExitStack
    with ExitStack() as st:
        def sb(name, shape, dt):
            return st.enter_context(nc.sbuf_tensor("sb_" + name, shape, dt))

        ident_f = sb("ident_f", [128, 128], F32)
        ident = sb("ident", [128, 128], BF16)
        ones_bf = sb("ones_bf", [128, 128], BF16)
        expA = sb("expA", [128, 8, 3, 128], BF16)
        expB = sb("expB", [128, 3, 4, 3, 128], BF16)
        gv = sb("gv", [128, 24], F32)
        gfb = sb("gfb", [128, D], F32)
        sink_sb = sb("sink_sb", [33, 8], F32)
        es_f = sb("es_f", [33, 8], F32)
        es_h32 = sb("es_h32", [33, 8], F32)
        es_hi8 = sb("es_hi8", [33, 8], BF16)
        es_lo8 = sb("es_lo8", [33, 8], BF16)
        es2 = sb("es2", [33, 8, 128], BF16)
        sinkV = sb("sinkV", [33, 128], BF16)
        eps_t = sb("eps_t", [128, 1], F32)
        stat = sb("stat", [128, 4, 4], F32)
        wring = sb("wring", [128, NSLOT, SLOT], BF16)
        xin = sb("xin", [128, 3, D], F32)
        hn = sb("hn", [128, 2, D], BF16)
        sqj = sb("sqj", [128, D], BF16)
        mkT = sb("mkT", [128, 4, NMEM], BF16)
        mv = sb("mv", [128, 2, 512], BF16)
        oA_T = sb("oA_T", [128, 4, S], BF16)
        oB_T = sb("oB_T", [128, 2, S], BF16)
        pbuf = sb("pbuf", [128, 12, 512], BF16)
        rden = sb("rden", [128, 2, 512], F32)
        arena = sb("arena", [128, 88 * 1024 // 2], BF16)
        ps = st.enter_context(nc.psum_tensor("ps", [128, 8, 512], F32))
        sems = {n: st.enter_context(nc.semaphore(n)) for n in sem_names}
        block = st.enter_context(nc.Block())

        def carve(off_bytes, shape, dt):
            n = int(np.prod(shape))
            esz = 2 if dt == BF16 else 4
            a = arena[:, off_bytes // 2: off_bytes // 2 + n * esz // 2]
            if dt != BF16:
                a = a.bitcast(dt)
            if len(shape) == 2:
                pat = "p (a b) -> p a b"
                return a.rearrange(pat, a=shape[0])
            if len(shape) == 3:
                return a.rearrange("p (a b c) -> p a b c", a=shape[0], b=shape[1])
            return a

        KB = 1024
        hT = carve(0, [8, S], BF16)
        qA_T = carve(32 * KB, [4, S], BF16)
        kA_T = carve(48 * KB, [1, S], BF16)
        vA = carve(52 * KB, [16, 2, 128], BF16)
        qB_T = carve(32 * KB, [1, S], BF16)
        kB_T = carve(36 * KB, [1, S], BF16)
        vB = carve(40 * KB, [16, 2, 128], BF16)
        accB = carve(48 * KB, [2, S], F32)
        hTbs = [carve(0, [8, TB], BF16), carve(80 * KB, [8, TB], BF16)]
        memT = carve(64 * KB, [8, NMEM], BF16)
        h2T = carve(8 * KB, [8, TB], BF16)
        x2 = carve(16 * KB, [4, D], F32)
        uT = carve(32 * KB, [32, TB], BF16)
        mergedT = carve(64 * KB, [8, TB], BF16)
        oM_T = carve(72 * KB, [4, TB], BF16)
        qM_T = carve(76 * KB, [4, TB], BF16)

        B_ps = [Buf() for _ in range(8)]
        B_w = [Buf() for _ in range(NSLOT)]
        B_xin = [Buf() for _ in range(3)]
        B_hn = [Buf() for _ in range(2)]
        B_stat = [Buf() for _ in range(4)]
        B_sqj = Buf()
        B_p = [Buf() for _ in range(12)]
        B_rden = [Buf() for _ in range(2)]
        B_memT, B_mkT, B_mv = Buf(), Buf(), Buf()
        B_oA = [Buf() for _ in range(16)]
        B_oB = Buf()
        B_hT = [Buf() for _ in range(16)]
        B_qA = [Buf() for _ in range(4)]
        B_kA = [Buf() for _ in range(4)]
        B_vA = [Buf() for _ in range(4)]
        B_qB, B_kB, B_vB, B_accB = Buf(), Buf(), Buf(), Buf()
        B_hTbs, B_h2T, B_uT, B_mergedT, B_oM, B_qM = [Buf(), Buf()], Buf(), [Buf() for _ in range(32)], [Buf() for _ in range(8)], Buf(), Buf()
        B_x2 = [Buf() for _ in range(4)]
        B_setup = Buf()

        rr = {"ps": 0, "w": 0, "xin": 0, "hn": 0, "stat": 0, "p": 0, "rden": 0}

        def nxt(k, n):
            i = rr[k]
            rr[k] = (i + 1) % n
            return i

        def bank():
            i = nxt("ps", 8)
            return ps[:, i, :], B_ps[i]

        def wload(blk, nelem=4096):
            i = nxt("w", NSLOT)
            csem = "d_c0" if blk in (BLK_A0, BLK_A1) else ("d_c1" if blk <= BLK_M0 else "d_c2")
            sc.run("sp", lambda e, i=i, blk=blk, nelem=nelem: e.dma_start(out=wring[:, i, 0:nelem], in_=wimg[blk, :, 0:nelem]),
                   writes=[B_w[i]], dsem="d_w%d" % i, extra=[(csem, sc.cnt[csem])])
            return wring[:, i, :], B_w[i]

        def wv(slot, off, k, n):
            return slot[:, off:off + k * n].rearrange("p (k n) -> p k n", k=k)

        sc.run("pool", [lambda e: e.memset(eps_t[:], 1e-6),
                        lambda e: e.memset(ones_bf[:], 1.0),
                        lambda e: e.memset(es2[:], 0.0),
                        lambda e: e.memset(sink_sb[:], 0.0),
                        lambda e: e.memset(stat[:], 0.0)], writes=[B_setup])
        B_sinkV = Buf()
        sc.run("pool", lambda e: e.memset(sinkV[:], 0.0), writes=[B_sinkV, B_setup])
        sc.run("pool", lambda e: e.memset(sinkV[0:1, 64:128], 1.0), writes=[B_sinkV, B_setup])
        sc.run("pool", lambda e: e.memset(sinkV[32:33, 64:128], 1.0), writes=[B_sinkV, B_setup])
        pending = {"d_c1": [], "d_c2": []}

        def cast(blk, off, src, k, n, csem, pace=None):
            if csem != "d_c0" and pace is None:
                pending[csem].append((blk, off, src, k, n))
                return
            sc.run("pool", lambda e: e.dma_start(
                out=wimg[blk, :, off:off + k * n].rearrange("p (k n) -> p k n", k=k),
                in_=src.rearrange("(k p) n -> p k n", p=128)), dsem=csem, extra=([pace] if pace else []))

        def issue_cast(csem, pace, n=1):
            for _ in range(n):
                if pending[csem]:
                    cast(*pending[csem].pop(0), csem, pace=pace)

        cast(BLK_A0, 0, w_in_d[:, 0:512], 8, 512, "d_c0")
        cast(BLK_A1, 0, w_in_d[:, 512:768], 8, 256, "d_c0")
        for i in range(6):
            cast(BLK_B + i, 0, w_in_d[:, 768 + 384 * i: 768 + 384 * (i + 1)], 8, 384, "d_c1")
        cast(BLK_KV0, 0, w_kv_d[:, 0:512], 8, 512, "d_c1")
        cast(BLK_KV1, 0, w_kv_d[:, 512:1024], 8, 512, "d_c1")
        cast(BLK_M0, 0, w_in_d[:, 3072:3584], 8, 512, "d_c1")
        for fc in range(8):
            cast(BLK_T1 + fc, 0, w_in_d[:, 3584 + 384 * fc: 3584 + 384 * (fc + 1)], 8, 384, "d_c2")
            cast(BLK_T1 + fc, 3072, w_a_d[:, 128 * fc:128 * (fc + 1)], 4, 128, "d_c2")
            cast(BLK_T1 + fc, 3584, w_b_d[:, 128 * fc:128 * (fc + 1)], 2, 128, "d_c2")
            cast(BLK_T1 + fc, 3840, w_m_d[:, 128 * fc:128 * (fc + 1)], 4, 128, "d_c2")
        for c in range(2):
            cast(BLK_O + c, 0, w_o_d[:, 512 * c:512 * (c + 1)], 8, 512, "d_c2")
        for fq in range(8):
            cast(BLK_U + fq, 0, w_up_d[:, 512 * fq:512 * (fq + 1)], 8, 512, "d_c2")
        for c in range(2):
            for kq in range(4):
                cast(BLK_D + c * 4 + kq, 0, w_dn_d[1024 * kq:1024 * (kq + 1), 512 * c:512 * (c + 1)], 8, 512, "d_c2")

        def sdma(out, in_):
            sc.run("sp", lambda e: e.dma_start(out=out, in_=in_), dsem="d_setup", writes=[B_setup])

        tA = carve(0, [8 * 3 * 128 // 128, 128], F32)
        tB = carve(16 * KB, [3 * 4 * 3 * 128 // 128, 128], F32)
        sdma(ident_f[:], ident_d[:, :])
        sdma(gv[:], gv_d[:, :])
        sdma(gfb[:], gf_d.partition_broadcast(128))
        sdma(sink_sb[0:1, :], sink_d[:, :])
        sdma(sink_sb[32:33, :], sink_d[:, :])
        sdma(tA, tabA_d.rearrange("p (a b) -> p a b", b=128))
        sdma(tB, tabB_d.rearrange("p (a b) -> p a b", b=128))
        sc.run("act", [lambda e: e.activation(out=ident[:], in_=ident_f[:], func=AF.Copy),
                       lambda e: e.activation(out=expA[:].rearrange("p a b c -> p (a b) c"), in_=tA, func=AF.Exp),
                       lambda e: e.activation(out=expB[:].rearrange("p a b c d -> p (a b c) d"), in_=tB, func=AF.Exp),
                       lambda e: e.activation(out=es_f[:], in_=sink_sb[:], func=AF.Exp)],
               reads=[B_setup], writes=[B_setup])
        sc.run("dve", lambda e: e.tensor_copy(out=es_hi8[:], in_=es_f[:]), reads=[B_setup], writes=[B_setup])
        sc.run("dve", lambda e: e.tensor_copy(out=es_h32[:], in_=es_hi8[:]), reads=[B_setup], writes=[B_setup])
        sc.run("dve", lambda e: e.tensor_tensor(out=es_lo8[:], in0=es_f[:], in1=es_h32[:], op=ALU.subtract),
               reads=[B_setup], writes=[B_setup])
        sc.run("dve", [lambda e: e.tensor_copy(out=es2[0:1, :, :], in_=es_hi8[0:1, :].unsqueeze(2).to_broadcast([1, 8, 128])),
                       lambda e: e.tensor_copy(out=es2[32:33, :, :], in_=es_lo8[32:33, :].unsqueeze(2).to_broadcast([1, 8, 128]))],
               reads=[B_setup], writes=[B_setup])
        sc.barrier(("pe", "act", "dve", "pool"))

        def rstd_of(src, src_bufs):
            si = nxt("stat", 4)
            sc.run("act", [lambda e: e.activation(out=sqj[:], in_=src, func=AF.Square, accum_out=stat[:, si, 0:1])],
                   reads=src_bufs, writes=[B_stat[si]])
            sc.run("act", lambda e: e.activation(out=stat[:, si, 1:2], in_=stat[:, si, 0:1], func=AF.Ln,
                                                 scale=1.0 / D, bias=eps_t[:, 0:1]), reads=[B_stat[si]], writes=[B_stat[si]])
            sc.run("act", lambda e: e.activation(out=stat[:, si, 2:3], in_=stat[:, si, 1:2], func=AF.Exp, scale=-0.5),
                   reads=[B_stat[si]], writes=[B_stat[si]])
            return stat[:, si, 2:3], B_stat[si]

        def norm_p1(src, src_bufs, scale_eng="act"):
            r_ap, r_buf = rstd_of(src, src_bufs)
            hi = nxt("hn", 2)
            if scale_eng == "act":
                sc.run("act", lambda e: e.activation(out=hn[:, hi, :], in_=src, func=AF.Copy, scale=r_ap),
                       reads=src_bufs + [r_buf], writes=[B_hn[hi]])
            else:
                sc.run("dve", lambda e: e.tensor_scalar(out=hn[:, hi, :], in0=src, scalar1=r_ap, scalar2=None, op0=ALU.mult),
                       reads=src_bufs + [r_buf], writes=[B_hn[hi]])
            return hi

        def norm_p2(hi, gcol, dsts):
            bk, bb = bank()
            bkb = bk.bitcast(BF16).rearrange("p (k t) -> p k t", k=8)
            sc.run("pe", [(lambda e, k=k: e.transpose(out=bkb[:, k, :], in_=hn[:, hi, 128 * k:128 * (k + 1)], identity=ident[:]))
                          for k in range(8)], reads=[B_hn[hi]], writes=[bb])
            gb = gv[:, gcol:gcol + 8].unsqueeze(2).to_broadcast([128, 8, 128])
            for (o, bufs) in dsts:
                sc.run("dve", lambda e, o=o: e.tensor_tensor(out=o, in0=bkb, in1=gb, op=ALU.mult), reads=[bb], writes=bufs)

        def norm_transpose(src, src_bufs, gcol, dsts):
            norm_p2(norm_p1(src, src_bufs), gcol, dsts)

        def xload(src):
            xi = nxt("xin", 3)
            sc.run("sp", lambda e: e.dma_start(out=xin[:, xi, :], in_=src), writes=[B_xin[xi]], dsem="d_x%d" % xi)
            return xin[:, xi, :], B_xin[xi]

        def proj_fm(slot, sbuf_, woff, wn, ncol0, rhs_fn, rhs_bufs, evac):
            w = wv(slot, woff, 8, wn)
            bk, bb = bank()
            sc.run("pe", [(lambda e, k=k: e.matmul(bk, lhsT=w[:, k, ncol0:ncol0 + 128], rhs=rhs_fn(k),
                                                   start=(k == 0), stop=(k == 7))) for k in range(8)],
                   reads=[sbuf_] + rhs_bufs, writes=[bb])
            evac(bk, bb)

        def copy_evac(eng, out, bufs):
            def f(bk, bb):
                if eng == "act":
                    sc.run("act", lambda e: e.activation(out=out, in_=bk, func=AF.Copy), reads=[bb], writes=bufs)
                else:
                    sc.run("dve", lambda e: e.tensor_copy(out=out, in_=bk), reads=[bb], writes=bufs)
            return f

        def do_seq(sq):
            _ck(0)
            pre_n1 = []
            for t in range(2):
                xa, xb_ = xload(x_d[sq, 128 * t:128 * (t + 1), :])
                pre_n1.append(norm_p1(xa, [xb_]))
            sc.barrier(("act", "dve", "pool"))
            def aproj_groups(tw, slotA0, sA0, slotA1, sA1, wA1):
                hb = B_hT[4 * tw:4 * tw + 4]
                rf = (lambda k: hT[:, k, 512 * tw:512 * (tw + 1)])
                groups = []
                for i in range(4):
                    groups.append(lambda i=i: proj_fm(slotA0, sA0, 0, 512, 128 * i, rf, hb,
                                                      copy_evac("dve", qA_T[:, i, 512 * tw:512 * (tw + 1)], [B_qA[tw]])))
                groups.append(lambda: proj_fm(slotA1, sA1, 0, 256, 0, rf, hb,
                                              copy_evac("dve", kA_T[:, 0, 512 * tw:512 * (tw + 1)], [B_kA[tw]])))

                def vgrp():
                    bk, bb = bank()
                    bk4 = bk.rearrange("p (t g d) -> p t g d", t=4, g=2)
                    fns = []
                    for tt in range(4):
                        t = 4 * tw + tt
                        fns += [(lambda e, k=k, t=t, tt=tt: e.matmul(bk4[:, tt, :, :], lhsT=hT[:, k, 128 * t:128 * (t + 1)],
                                                                     rhs=wA1[:, k, 128:256], start=(k == 0), stop=(k == 7)))
                                for k in range(8)]
                    sc.run("pe", fns, reads=[sA1] + hb, writes=[bb])
                    sc.run("dve", lambda e: e.tensor_copy(out=vA[:, 4 * tw:4 * tw + 4, :, 0:64], in_=bk4), reads=[bb], writes=[B_vA[tw]])
                groups.append(vgrp)
                return groups

            sc.run("pool", lambda e: e.memset(vA[:, :, :, 64:128], 1.0), writes=B_vA)
            pend = []
            for tw in range(4):
                for t in range(4 * tw, 4 * tw + 4):
                    if t < 2:
                        hi_ = pre_n1[t]
                    else:
                        xa, xb_ = xload(x_d[sq, 128 * t:128 * (t + 1), :])
                        hi_ = norm_p1(xa, [xb_], scale_eng=("dve" if t % 2 else "act"))
                    norm_p2(hi_, 0, [(hT[:, :, 128 * t:128 * (t + 1)], [B_hT[t]])])
                    if t >= 3:
                        issue_cast("d_c1", ("s_act", sc.cnt["s_act"]))
                    for _ in range(2):
                        if pend:
                            pend.pop(0)()
                if tw == 3:
                    issue_cast("d_c1", ("s_act", sc.cnt["s_act"]), n=100)
                if tw == 0:
                    slotA0, sA0 = wload(BLK_A0)
                    slotA1, sA1 = wload(BLK_A1, 8 * 256)
                    wA1 = wv(slotA1, 0, 8, 256)
                pend += aproj_groups(tw, slotA0, sA0, slotA1, sA1, wA1)
            while pend:
                pend.pop(0)()
            _ck(1)
            _ck(2)
            for mt in range(2):
                xa, xb_ = xload(mem_d[sq, 128 * mt:128 * (mt + 1), :])
                norm_transpose(xa, [xb_], 16, [(memT[:, :, 128 * mt:128 * (mt + 1)], [B_memT])])
            slot, sbuf_ = wload(BLK_KV0)
            w = wv(slot, 0, 8, 512)
            for hp in range(2):
                bk, bb = bank()
                bk2 = bk.rearrange("p (a b) -> p a b", a=2)
                fns = []
                for hh in range(2):
                    h = 2 * hp + hh
                    fns += [(lambda e, k=k, h=h, hh=hh, bk2=bk2, w=w: e.matmul(bk2[:, hh, :], lhsT=w[:, k, 128 * h:128 * (h + 1)], rhs=memT[:, k, :],
                                                                 start=(k == 0), stop=(k == 7))) for k in range(8)]
                sc.run("pe", fns, reads=[sbuf_, B_memT], writes=[bb])
                sc.run("act", lambda e, hp=hp, bk2=bk2: e.activation(out=mkT[:, 2 * hp:2 * hp + 2, :], in_=bk2, func=AF.Copy),
                       reads=[bb], writes=[B_mkT])
            slot, sbuf_ = wload(BLK_KV1)
            w = wv(slot, 0, 8, 512)
            for mt in range(2):
                bk, bb = bank()
                sc.run("pe", [(lambda e, k=k, mt=mt, bk=bk, w=w: e.matmul(bk, lhsT=memT[:, k, 128 * mt:128 * (mt + 1)], rhs=w[:, k, :],
                                                                          start=(k == 0), stop=(k == 7))) for k in range(8)],
                       reads=[sbuf_, B_memT], writes=[bb])
                sc.run("act", lambda e, mt=mt, bk=bk: e.activation(out=mv[:, mt, :], in_=bk, func=AF.Copy), reads=[bb], writes=[B_mv])

            _ck(3)
            stepsA = list(range(16))

            def A_scores(qb):
                kts = [kt for kt in (qb - 1, qb, qb + 1) if 0 <= kt <= 15]
                banks = {(kt, g): bank() for kt in kts for g in range(2)}
                for kt in kts:
                    for g in range(2):
                        bk, bb = banks[(kt, g)]
                        pl = slice(64 * g, 64 * g + 64)
                        sc.run("pe", lambda e, bk=bk, kt=kt, pl=pl: e.matmul(
                            bk, lhsT=kA_T[pl, 0, 128 * kt:128 * (kt + 1)], rhs=qA_T[pl, :, 128 * qb:128 * (qb + 1)], start=True, stop=True),
                            reads=[B_kA[kt // 4], B_qA[qb // 4]], writes=[bb])
                res = {0: [], 1: []}
                for kt in kts:
                    for g in range(2):
                        bk, bb = banks[(kt, g)]
                        pi = nxt("p", 12)
                        sc.run("act", lambda e, bk=bk, pi=pi: e.activation(out=pbuf[:, pi, :], in_=bk, func=AF.Exp, scale=0.125),
                               reads=[bb], writes=[B_p[pi]])
                        sc.run("dve", lambda e, pi=pi, kt=kt, g=g: e.tensor_tensor(
                            out=pbuf[:, pi, :].rearrange("p (h q) -> p h q", h=4), in0=pbuf[:, pi, :].rearrange("p (h q) -> p h q", h=4),
                            in1=expA[:, 4 * g:4 * g + 4, kt - qb + 1, :], op=ALU.mult), writes=[B_p[pi]])
                        res[g].append((kt, pi))
                return res

            def A_pv(qb, g, res):
                bk, bb = bank()
                fns = []
                for n, (kt, pi) in enumerate(res):
                    fns.append(lambda e, kt=kt, pi=pi, n=n: e.matmul(bk, lhsT=vA[:, kt, g, :], rhs=pbuf[:, pi, :], start=(n == 0), stop=False))
                fns.append(lambda e: e.matmul(bk, lhsT=sinkV[0:33, :], rhs=es2[0:33, 4 * g:4 * g + 4, :], start=False, stop=True))
                kts = [kt for kt, _ in res]
                sc.run("pe", fns, reads=[B_p[pi] for _, pi in res] + [B_vA[kt // 4] for kt in kts], writes=[bb])
                ri = nxt("rden", 2)
                sc.run("act", lambda e: e.activation(out=rden[0:64, ri, :], in_=bk[64:128, :], func=AF.Ln), reads=[bb], writes=[B_rden[ri]])
                sc.run("act", lambda e: e.activation(out=rden[0:64, ri, :], in_=rden[0:64, ri, :], func=AF.Exp, scale=-1.0),
                       reads=[B_rden[ri]], writes=[B_rden[ri]])
                bk4 = bk.rearrange("p (h q) -> p h q", h=4)
                rd4 = rden[:, ri, :].rearrange("p (h q) -> p h q", h=4)
                qs = slice(128 * qb, 128 * (qb + 1))
                sc.run("dve", [lambda e: e.tensor_tensor(out=oA_T[0:64, 2 * g:2 * g + 2, qs], in0=bk4[0:64, 0:4:2, :],
                                                         in1=rd4[0:64, 0:4:2, :], op=ALU.mult),
                               lambda e: e.tensor_tensor(out=oA_T[64:128, 2 * g:2 * g + 2, qs], in0=bk4[0:64, 1:4:2, :],
                                                         in1=rd4[0:64, 1:4:2, :], op=ALU.mult)],
                       reads=[bb, B_rden[ri]], writes=[B_oA[qb]])

            prev = None
            for n in range(len(stepsA) + 1):
                cur = None
                if n < len(stepsA):
                    cur = A_scores(stepsA[n])
                if prev is not None:
                    for g in range(2):
                        A_pv(stepsA[n - 1], g, prev[g])
                        issue_cast("d_c2", ("s_pe", sc.cnt["s_pe"]))
                prev = cur

            def hTb_of(tw):
                return hTbs[(tw + 1) % 2], B_hTbs[(tw + 1) % 2]

            def t_hTb_p1(tw, tt):
                tok0 = 512 * tw
                xa, xb_ = xload(x_d[sq, tok0 + 128 * tt: tok0 + 128 * (tt + 1), :])
                return norm_p1(xa, [xb_])

            def t_hTb_p2(tw, tt, hi):
                hb, hbuf = hTb_of(tw)
                norm_p2(hi, 0, [(hb[:, :, 128 * tt:128 * (tt + 1)], [hbuf])])

            def t_Mproj(tw):
                hb, hbuf = hTb_of(tw)
                slotM, sM = wload(BLK_M0)
                for h in range(4):
                    proj_fm(slotM, sM, 0, 512, 128 * h, (lambda k: hb[:, k, :]), [hbuf],
                            copy_evac("act", qM_T[:, h, :], [B_qM]))

            def t_MS(h):
                pis = []
                for mt in range(2):
                    bk, bb = bank()
                    sc.run("pe", lambda e, bk=bk, mt=mt: e.matmul(bk, lhsT=mkT[:, h, 128 * mt:128 * (mt + 1)], rhs=qM_T[:, h, :],
                                                                  start=True, stop=True), reads=[B_mkT, B_qM], writes=[bb])
                    pi = nxt("p", 12)
                    sc.run("act", lambda e, bk=bk, pi=pi: e.activation(out=pbuf[:, pi, :], in_=bk, func=AF.Exp, scale=128 ** -0.5),
                           reads=[bb], writes=[B_p[pi]])
                    pis.append(pi)
                return pis

            def t_MPV(h, pis):
                bkO, bbO = bank()
                bkD, bbD = bank()
                sc.run("pe", [(lambda e, mt=mt: e.matmul(bkO, lhsT=mv[:, mt, 128 * h:128 * (h + 1)], rhs=pbuf[:, pis[mt], :],
                                                         start=(mt == 0), stop=(mt == 1))) for mt in range(2)],
                       reads=[B_p[p] for p in pis] + [B_mv], writes=[bbO])
                sc.run("pe", [(lambda e, mt=mt: e.matmul(bkD, lhsT=ones_bf[:], rhs=pbuf[:, pis[mt], :],
                                                         start=(mt == 0), stop=(mt == 1))) for mt in range(2)],
                       reads=[B_p[p] for p in pis], writes=[bbD])
                ri = nxt("rden", 2)
                sc.run("act", lambda e: e.activation(out=rden[:, ri, :], in_=bkD, func=AF.Ln), reads=[bbD], writes=[B_rden[ri]])
                sc.run("act", lambda e: e.activation(out=rden[:, ri, :], in_=rden[:, ri, :], func=AF.Exp, scale=-1.0),
                       reads=[B_rden[ri]], writes=[B_rden[ri]])
                sc.run("dve", lambda e: e.tensor_tensor(out=oM_T[:, h, :], in0=bkO, in1=rden[:, ri, :], op=ALU.mult),
                       reads=[bbO, B_rden[ri]], writes=[B_oM])

            def t_Mheads():
                prevp = None
                for h in range(5):
                    cur = t_MS(h) if h < 4 else None
                    if prevp is not None:
                        t_MPV(h - 1, prevp)
                    prevp = cur

            def t_T1fc(tw, fc):
                tok0 = 512 * tw
                hb, hbuf = hTb_of(tw)
                sg = pbuf
                slotT, sT = wload(BLK_T1 + fc, SLOT)
                wg = wv(slotT, 0, 8, 384)
                wa = wv(slotT, 3072, 4, 128)
                wb_ = wv(slotT, 3584, 2, 128)
                wm = wv(slotT, 3840, 4, 128)
                gis = []
                for gi in range(3):
                    bk, bb = bank()
                    sc.run("pe", [(lambda e, k=k, bk=bk, gi=gi: e.matmul(bk, lhsT=wg[:, k, 128 * gi:128 * (gi + 1)], rhs=hb[:, k, :],
                                                                        start=(k == 0), stop=(k == 7))) for k in range(8)],
                           reads=[sT, hbuf], writes=[bb])
                    pi = nxt("p", 12)
                    sc.run("act", lambda e, bk=bk, pi=pi: e.activation(out=sg[:, pi, :], in_=bk, func=AF.Sigmoid), reads=[bb], writes=[B_p[pi]])
                    gis.append(pi)
                prods = []
                for bi, (wbr, nk, src, sbufs) in enumerate(((wa, 4, oA_T, B_oA[4 * tw:4 * tw + 4]), (wb_, 2, oB_T, [B_oB]), (wm, 4, oM_T, [B_oM]))):
                    bk, bb = bank()
                    if bi == 2:
                        rf = (lambda k, src=src: src[:, k, :])
                    else:
                        rf = (lambda k, src=src: src[:, k, tok0:tok0 + 512])
                    sc.run("pe", [(lambda e, k=k, bk=bk, wbr=wbr, rf=rf, nk=nk: e.matmul(bk, lhsT=wbr[:, k, :], rhs=rf(k),
                                                                                      start=(k == 0), stop=(k == nk - 1))) for k in range(nk)],
                           reads=[sT] + list(sbufs), writes=[bb])
                    prods.append((bk, bb))
                sc.run("dve", lambda e, b=prods[0][0], p=gis[0]: e.tensor_tensor(out=rden[:, 0, :], in0=b, in1=sg[:, p, :], op=ALU.mult),
                       reads=[prods[0][1], B_p[gis[0]]], writes=[B_rden[0]])
                sc.run("dve", lambda e, b=prods[1][0], p=gis[1]: e.tensor_tensor(out=rden[:, 1, :], in0=b, in1=sg[:, p, :], op=ALU.mult),
                       reads=[prods[1][1], B_p[gis[1]]], writes=[B_rden[1]])
                sc.run("dve", lambda e: e.tensor_tensor(out=rden[:, 0, :], in0=rden[:, 0, :], in1=rden[:, 1, :], op=ALU.add),
                       reads=[B_rden[1]], writes=[B_rden[0]])
                sc.run("dve", lambda e, b=prods[2][0], p=gis[2]: e.tensor_tensor(out=rden[:, 1, :], in0=b, in1=sg[:, p, :], op=ALU.mult),
                       reads=[prods[2][1], B_p[gis[2]]], writes=[B_rden[1]])
                sc.run("dve", lambda e: e.tensor_tensor(out=mergedT[:, fc, :], in0=rden[:, 0, :], in1=rden[:, 1, :], op=ALU.add),
                       reads=[B_rden[0], B_rden[1]], writes=[B_mergedT[fc]])

            def t_T2a(tw, xpre):
                tok0 = 512 * tw
                slots = [wload(BLK_O + c) for c in range(2)]
                wos = [wv(sl, 0, 8, 512) for sl, _ in slots]
                xall = [bank() for _ in range(8)]
                for k in range(8):
                    for c in range(2):
                        for tt in range(4):
                            bk, bb = xall[c * 4 + tt]
                            sc.run("pe", lambda e, k=k, bk=bk, tt=tt, wo=wos[c]: e.matmul(
                                bk, lhsT=mergedT[:, k, 128 * tt:128 * (tt + 1)], rhs=wo[:, k, :], start=(k == 0), stop=(k == 7)),
                                reads=[slots[c][1], B_mergedT[k]], writes=[bb])
                xs = list(xpre)
                for tt in range(4):
                    if tt >= 2:
                        xa, xb_ = xload(x_d[sq, tok0 + 128 * tt: tok0 + 128 * (tt + 1), :])
                    else:
                        xa, xb_ = xs[tt]
                    for c in range(2):
                        bk, bb = xall[c * 4 + tt]
                        sc.run("dve", lambda e, bk=bk, tt=tt, c=c, xa=xa: e.tensor_tensor(out=x2[:, tt, 512 * c:512 * (c + 1)], in0=bk,
                                                                                        in1=xa[:, 512 * c:512 * (c + 1)], op=ALU.add),
                               reads=[bb, xb_], writes=[B_x2[tt]])

            def t_n2_p1(tt):
                return norm_p1(x2[:, tt, :], [B_x2[tt]])

            def t_n2_p2(tt, hi):
                norm_p2(hi, 8, [(h2T[:, :, 128 * tt:128 * (tt + 1)], [B_h2T])])

            def t_T3(tw):
                tok0 = 512 * tw
                for fq in range(8):
                    slotU, sU = wload(BLK_U + fq)
                    wu = wv(slotU, 0, 8, 512)
                    for fcc in range(4):
                        bk, bb = bank()
                        sc.run("pe", [(lambda e, k=k, bk=bk, fcc=fcc, wu=wu: e.matmul(bk, lhsT=wu[:, k, 128 * fcc:128 * (fcc + 1)], rhs=h2T[:, k, :],
                                                                                   start=(k == 0), stop=(k == 7))) for k in range(8)],
                               reads=[sU, B_h2T], writes=[bb])
                        ch = 4 * fq + fcc
                        sc.run("act", lambda e, bk=bk, ch=ch: e.activation(out=uT[:, ch, :], in_=bk, func=AF.Relu), reads=[bb], writes=[B_uT[ch]])
                    sc.run("dve", lambda e, fq=fq: e.tensor_tensor(out=uT[:, 4 * fq:4 * fq + 4, :], in0=uT[:, 4 * fq:4 * fq + 4, :],
                                                                   in1=uT[:, 4 * fq:4 * fq + 4, :], op=ALU.mult),
                           writes=B_uT[4 * fq:4 * fq + 4])
                for c in range(2):
                    banks = [bank() for _ in range(4)]
                    for kq in range(4):
                        slotD, sD = wload(BLK_D + c * 4 + kq)
                        wd = wv(slotD, 0, 8, 512)
                        for tt in range(4):
                            bk, bb = banks[tt]
                            sc.run("pe", [(lambda e, k=k, bk=bk, tt=tt, wd=wd, kq=kq: e.matmul(
                                bk, lhsT=uT[:, 8 * kq + k, 128 * tt:128 * (tt + 1)], rhs=wd[:, k, :],
                                start=(kq == 0 and k == 0), stop=(kq == 3 and k == 7))) for k in range(8)],
                                reads=[sD] + B_uT[8 * kq:8 * kq + 8], writes=[bb])
                    for tt in range(4):
                        bk, bb = banks[tt]
                        sc.run("dve", lambda e, bk=bk, tt=tt, c=c: e.tensor_tensor(out=x2[:, tt, 512 * c:512 * (c + 1)], in0=bk,
                                                                               in1=x2[:, tt, 512 * c:512 * (c + 1)], op=ALU.add),
                               reads=[bb], writes=[B_x2[tt]])
                for tt in range(4):
                    r_ap, r_buf = rstd_of(x2[:, tt, :], [B_x2[tt]])
                    sc.run("dve", lambda e, tt=tt, r_ap=r_ap: e.scalar_tensor_tensor(out=x2[:, tt, :], in0=x2[:, tt, :], scalar=r_ap, in1=gfb[:],
                                                                                    op0=ALU.mult, op1=ALU.mult),
                           reads=[r_buf], writes=[B_x2[tt]])
                    sc.run("pool", lambda e, tt=tt: e.dma_start(out=y_d[sq, tok0 + 128 * tt: tok0 + 128 * (tt + 1), :], in_=x2[:, tt, :]),
                           reads=[B_x2[tt]], dsem="d_st%d" % tt)

            _ck(4)
            sc.barrier()
            def do_B(jp, g, dil):
                if True:
                    Lg = S // dil
                    nqb = Lg // 128
                    slotB, sB = wload(BLK_B + 2 * g + jp, 8 * 384)
                    wB = wv(slotB, 0, 8, 384)
                    for tw in range(4):
                        hb = B_hT[4 * tw:4 * tw + 4]
                        for which, dst, dbuf in ((0, qB_T, B_qB), (1, kB_T, B_kB)):
                            bk, bb = bank()
                            sc.run("pe", [(lambda e, k=k, bk=bk, which=which, tw=tw: e.matmul(
                                bk, lhsT=wB[:, k, 128 * which:128 * (which + 1)], rhs=hT[:, k, 512 * tw:512 * (tw + 1)],
                                start=(k == 0), stop=(k == 7))) for k in range(8)], reads=[sB] + hb, writes=[bb])
                            if dil == 1:
                                o = dst[:, 0, 512 * tw:512 * (tw + 1)]
                                i_ = bk
                            else:
                                m = 512 // dil
                                o = dst[:, 0, :].rearrange("p (r b) -> p r b", r=dil)[:, :, m * tw:m * (tw + 1)]
                                i_ = bk.rearrange("p (m r) -> p r m", r=dil)
                            eng = "act" if which == 0 else "dve"
                            if eng == "act":
                                sc.run("act", lambda e, o=o, i_=i_: e.activation(out=o, in_=i_, func=AF.Copy), reads=[bb], writes=[dbuf])
                            else:
                                sc.run("dve", lambda e, o=o, i_=i_: e.tensor_copy(out=o, in_=i_), reads=[bb], writes=[dbuf])
                    sc.run("pool", lambda e: e.memset(vB[:, :, :, 64:128], 1.0), writes=[B_vB])
                    for t4 in range(4):
                        bk, bb = bank()
                        bk4 = bk.rearrange("p (t j d) -> p t j d", t=4, j=2)
                        fns = []
                        for tt in range(4):
                            tl = 4 * t4 + tt
                            r, qb = tl // nqb, tl % nqb
                            st0 = dil * 128 * qb + r
                            fns += [(lambda e, k=k, tt=tt, st0=st0, bk4=bk4: e.matmul(
                                bk4[:, tt, :, :], lhsT=hT[:, k, st0:st0 + 127 * dil + 1:dil], rhs=wB[:, k, 256:384],
                                start=(k == 0), stop=(k == 7))) for k in range(8)]
                        sc.run("pe", fns, reads=[sB] + B_hT, writes=[bb])
                        sc.run("dve", lambda e, t4=t4, bk4=bk4: e.tensor_copy(out=vB[:, 4 * t4:4 * t4 + 4, :, 0:64], in_=bk4),
                               reads=[bb], writes=[B_vB])
                    if STAGE == 54:
                        return
                    stepsB = [(r, qb) for r in range(dil) for qb in range(nqb)]

                    def B_scores(r, qb):
                        kts = [kt for kt in (qb - 1, qb, qb + 1) if 0 <= kt < nqb]
                        b0 = kts[0] - qb + 1
                        nb = len(kts)
                        qc = r * Lg + 128 * qb
                        pis = []
                        bks = [bank(), bank()]
                        sc.run("pe", [(lambda e, jj=jj, kt=kt: e.matmul(
                            bks[jj][0][:, 128 * (kt - qb + 1):128 * (kt - qb + 2)],
                            lhsT=kB_T[64 * jj:64 * jj + 64, 0, r * Lg + 128 * kt:r * Lg + 128 * kt + 128],
                            rhs=qB_T[64 * jj:64 * jj + 64, 0, qc:qc + 128], start=True, stop=True)) for kt in kts for jj in range(2)],
                            reads=[B_kB, B_qB], writes=[bks[0][1], bks[1][1]])
                        cs = slice(128 * b0, 128 * (b0 + nb))
                        for jj in range(2):
                            bk, bb = bks[jj]
                            pi = nxt("p", 12)
                            sc.run("act", lambda e, bk=bk, pi=pi: e.activation(out=pbuf[:, pi, cs], in_=bk[:, cs], func=AF.Exp, scale=0.125),
                                   reads=[bb], writes=[B_p[pi]])
                            sc.run("dve", lambda e, pi=pi, jj=jj: e.tensor_tensor(
                                out=pbuf[:, pi, cs].rearrange("p (b q) -> p b q", b=nb),
                                in0=pbuf[:, pi, cs].rearrange("p (b q) -> p b q", b=nb),
                                in1=expB[:, g, 2 * jp + jj, b0:b0 + nb, :], op=ALU.mult), writes=[B_p[pi]])
                            pis.append(pi)
                        return (kts, pis)

                    def B_pv(r, qb, res):
                        kts, pis = res
                        bk, bb = bank()
                        bk2 = bk[:, 0:256].rearrange("p (j q) -> p j q", j=2)
                        fns = []
                        for jj in range(2):
                            for n, kt in enumerate(kts):
                                b = kt - qb + 1
                                fns.append(lambda e, kt=kt, n=n, jj=jj, b=b: e.matmul(
                                    bk2[:, jj, :], lhsT=vB[:, r * nqb + kt, jj, :], rhs=pbuf[:, pis[jj], 128 * b:128 * (b + 1)],
                                    start=(n == 0), stop=(n == len(kts) - 1)))
                        sc.run("pe", fns, reads=[B_p[pi] for pi in pis] + [B_vB], writes=[bb])
                        if dil == 1:
                            sc.run("dve", lambda e: e.tensor_copy(out=accB[:, :, 128 * qb:128 * (qb + 1)], in_=bk2), reads=[bb], writes=[B_accB])
                        else:
                            st0 = dil * 128 * qb + r
                            a = accB[:, :, st0:st0 + 127 * dil + 1:dil]
                            sc.run("dve", lambda e: e.tensor_tensor(out=a, in0=bk2, in1=a, op=ALU.add), reads=[bb], writes=[B_accB])

                    LA = 3
                    resq = []
                    for n in range(len(stepsB) + LA):
                        if n < len(stepsB):
                            resq.append(B_scores(*stepsB[n]))
                        if n >= LA and STAGE not in (55, 56, 57):
                            B_pv(*stepsB[n - LA], resq[n - LA])
                            issue_cast("d_c2", ("s_pe", sc.cnt["s_pe"]))

            def do_Bcomb(jp):
                for w4 in range(4):
                    cs = slice(512 * w4, 512 * (w4 + 1))
                    for jj in range(2):
                        ri = nxt("rden", 2)
                        sc.run("act", lambda e, ri=ri, jj=jj, cs=cs: e.activation(out=rden[0:64, ri, :], in_=accB[64:128, jj, cs], func=AF.Ln),
                               reads=[B_accB], writes=[B_rden[ri]])
                        sc.run("act", lambda e, ri=ri: e.activation(out=rden[0:64, ri, :], in_=rden[0:64, ri, :], func=AF.Exp, scale=-1.0),
                               reads=[B_rden[ri]], writes=[B_rden[ri]])
                        sc.run("dve", lambda e, ri=ri, jj=jj, cs=cs: e.tensor_tensor(
                            out=oB_T[64 * jj:64 * jj + 64, jp, cs], in0=accB[0:64, jj, cs], in1=rden[0:64, ri, :], op=ALU.mult),
                            reads=[B_accB, B_rden[ri]], writes=[B_oB])

            pro = {}
            for jp in range(2):
                for g, dil in enumerate((1, 4, 16)):
                    if (STAGE in (50, 54, 55, 56, 57) and g > 0) or (STAGE == 51 and g > 1) or (STAGE == 53 and g != 2):
                        continue
                    if jp == 1 and STAGE == 99:
                        if g == 0:
                            pro["h"] = [t_hTb_p1(0, 0), t_hTb_p1(0, 1)]
                        elif g == 1:
                            t_hTb_p2(0, 0, pro["h"][0])
                            t_hTb_p2(0, 1, pro["h"][1])
                            pro["h"] = [t_hTb_p1(0, 2), t_hTb_p1(0, 3)]
                        else:
                            t_hTb_p2(0, 2, pro["h"][0])
                            t_hTb_p2(0, 3, pro["h"][1])
                            t_Mproj(0)
                    do_B(jp, g, dil)
                if STAGE in (50, 51, 52, 53, 54, 55, 56, 57):
                    continue
                do_Bcomb(jp)
            issue_cast("d_c2", ("s_pe", sc.cnt["s_pe"]), n=100)
            if STAGE == 99:
                t_Mheads()
            _ck(50)
            _ck(54)
            _ck(56)
            _ck(57)
            _ck(55)
            _ck(51)
            _ck(52)
            _ck(53)

            _ck(5)

            def t_T1(b, hooks_at):
                nb = b + 1 if b < 3 else None
                st = {}
                for fc in range(8):
                    t_T1fc(b, fc)
                    if nb is None:
                        continue
                    if fc == hooks_at[0]:
                        st["h"] = [t_hTb_p1(nb, 0), t_hTb_p1(nb, 1)]
                    elif fc == hooks_at[1]:
                        t_hTb_p2(nb, 0, st["h"][0])
                        t_hTb_p2(nb, 1, st["h"][1])
                        st["h"] = [t_hTb_p1(nb, 2), t_hTb_p1(nb, 3)]
                    elif fc == hooks_at[2]:
                        t_hTb_p2(nb, 2, st["h"][0])
                        t_hTb_p2(nb, 3, st["h"][1])

            t_T1(0, (0, 2, 4))
            for tw in range(4):
                nb = tw + 1 if tw < 3 else None
                if nb is not None:
                    t_Mproj(nb)
                xpre = [xload(x_d[sq, 512 * tw + 128 * tt: 512 * tw + 128 * (tt + 1), :]) for tt in range(2)]
                t_T2a(tw, xpre)
                if nb is not None:
                    t_Mheads()
                h2 = [t_n2_p1(0), t_n2_p1(1)]
                if nb is not None:
                    t_T1fc(nb, 0)
                t_n2_p2(0, h2[0])
                t_n2_p2(1, h2[1])
                h2 = [t_n2_p1(2), t_n2_p1(3)]
                if nb is not None:
                    t_T1fc(nb, 1)
                t_n2_p2(2, h2[0])
                t_n2_p2(3, h2[1])
                if nb is not None:
                    nnb = nb + 1 if nb < 3 else None
                    st = {}
                    for fc in range(2, 8):
                        t_T1fc(nb, fc)
                        if nnb is None:
                            continue
                        if fc == 2:
                            st["h"] = [t_hTb_p1(nnb, 0), t_hTb_p1(nnb, 1)]
                        elif fc == 4:
                            t_hTb_p2(nnb, 0, st["h"][0])
                            t_hTb_p2(nnb, 1, st["h"][1])
                            st["h"] = [t_hTb_p1(nnb, 2), t_hTb_p1(nnb, 3)]
                        elif fc == 6:
                            t_hTb_p2(nnb, 2, st["h"][0])
                            t_hTb_p2(nnb, 3, st["h"][1])
                t_T3(tw)

        try:
            for sq in range(NSEQ):
                do_seq(sq)
        except _Stop:
            pass
        for tt in range(4):
            sc.wait("pool", "d_st%d" % tt, sc.cnt.get("d_st%d" % tt, 0))
            sc.wait("sp", "d_st%d" % tt, sc.cnt.get("d_st%d" % tt, 0))

        @block.tensor
        def _(e):
            sc.replay("pe", e, sems)

        @block.scalar
        def _(e):
            sc.replay("act", e, sems)

        @block.vector
        def _(e):
            sc.replay("dve", e, sems)

        @block.gpsimd
        def _(e):
            sc.replay("pool", e, sems)

        @block.sync
        def _(e):
            sc.replay("sp", e, sems)
    return nc


_NC_CACHE = {}


def _get_nc(nseq):
    if nseq not in _NC_CACHE:
        _NC_CACHE[nseq] = build(nseq)
    return _NC_CACHE[nseq]


def _shared_inputs(rel_bias, norm1_g, w_in, mem_norm_g, w_mem_kv, sink_logit, w_branch_a, w_branch_b,
                   w_branch_m, w_out, norm2_g, w_up, w_down, final_norm_g):
    f = lambda a: np.ascontiguousarray(np.asarray(a, dtype=np.float32))
    tabA, tabB = _bias_tables(f(rel_bias))
    gv = np.concatenate([f(norm1_g)[0].reshape(8, 128).T, f(norm2_g)[0].reshape(8, 128).T,
                         f(mem_norm_g)[0].reshape(8, 128).T], axis=1)
    return {
        "w_in_p": f(f(w_in)[0][:, _w_in_perm()]),
        "w_kv": f(w_mem_kv)[0], "w_a": f(w_branch_a)[0], "w_b": f(w_branch_b)[0], "w_m": f(w_branch_m)[0],
        "w_o": f(w_out)[0], "w_up": f(w_up)[0], "w_dn": f(w_down)[0],
        "tabA": f(tabA), "tabB": f(tabB), "gv": f(gv), "gf": f(final_norm_g).reshape(1, D),
        "sink": f(sink_logit).reshape(1, 8), "ident": np.eye(128, dtype=np.float32),
    }


def kernel(x_prompt, x_sample, mem_prompt, mem_sample, rel_bias, norm1_g, w_in, mem_norm_g, w_mem_kv,
           sink_logit, w_branch_a, w_branch_b, w_branch_m, w_out, norm2_g, w_up, w_down, final_norm_g):
    xp = np.asarray(x_prompt, dtype=np.float32)
    xs = np.asarray(x_sample, dtype=np.float32)
    mp = np.asarray(mem_prompt, dtype=np.float32)
    ms = np.asarray(mem_sample, dtype=np.float32)
    shared = _shared_inputs(rel_bias, norm1_g, w_in, mem_norm_g, w_mem_kv, sink_logit, w_branch_a, w_branch_b,
                            w_branch_m, w_out, norm2_g, w_up, w_down, final_norm_g)
    nseq = 5
    nc = _get_nc(nseq)
    in_maps = []
    for c in range(NCORES):
        m = dict(shared)
        m["x"] = np.ascontiguousarray(np.concatenate([xp[4 * c:4 * c + 4], xs[c:c + 1]], axis=0))
        m["mem"] = np.ascontiguousarray(np.concatenate([mp[4 * c:4 * c + 4], ms[c:c + 1]], axis=0))
        in_maps.append(m)
    res = run_bass_kernel_spmd(nc, in_maps, core_ids=list(range(NCORES)))
    yp = np.empty_like(xp)
    ys = np.empty_like(xs)
    for c in range(NCORES):
        y = res.results[c]["y"]
        yp[4 * c:4 * c + 4] = y[0:4]
        ys[c] = y[4]
    return (yp, ys)
```

```python
import numpy as np
import concourse.bass as bass
import concourse.mybir as mybir
from concourse.bass_utils import run_bass_kernel_spmd

F32 = mybir.dt.float32
BF16 = mybir.dt.bfloat16
AF = mybir.ActivationFunctionType
ALU = mybir.AluOpType

D = 1024
S = 2048
NMEM = 256
DFF = 4096
NCORES = 8
SLOT = 4352
NSLOT = 4
TB = 512
NEGM = -30000.0

BLK_KV0, BLK_KV1, BLK_A0, BLK_A1 = 0, 1, 2, 3
BLK_B = 4
BLK_M0 = 10
BLK_T1 = 11
BLK_O = 19
BLK_U = 21
BLK_D = 29
NBLK = 37


def _t5_bucket(rel):
    half = 16
    ret = (rel > 0).astype(np.int32) * half
    n = np.abs(rel)
    max_exact = half // 2
    large = max_exact + (np.log(np.maximum(n, 1) / max_exact) / np.log(1024 / max_exact)
                         * (half - max_exact)).astype(np.int32)
    large = np.minimum(large, half - 1)
    return (ret + np.where(n < max_exact, n, large)).astype(np.int32)


def _w_in_perm():
    cols = []
    for i in range(4):
        cols += list(range(64 * i, 64 * i + 64)) + list(range(256 + 64 * i, 256 + 64 * i + 64))
    cols += list(range(512, 768))
    for g in range(3):
        for jp in range(2):
            h0 = 4 * g + 2 * jp
            for base in (768, 1536, 2304):
                cols += list(range(base + 64 * h0, base + 64 * h0 + 128))
    cols += list(range(3072, 3584))
    for fc in range(8):
        for base in (3584, 4608, 5632):
            cols += list(range(base + 128 * fc, base + 128 * fc + 128))
    assert len(cols) == 6656 and len(set(cols)) == 6656
    return np.array(cols)


def _bias_tables(rel_bias):
    kp = np.arange(128)[:, None]
    qp = np.arange(128)[None, :]
    tabA = np.full((128, 8, 3, 128), NEGM, np.float32)
    for b in range(3):
        off = (b - 1) * 128 + kp - qp
        valid = np.abs(off) <= 128
        bk = _t5_bucket(np.clip(off, -128, 128))
        for h in range(8):
            tabA[:, h, b, :] = np.where(valid, rel_bias[bk, h], NEGM)
    tabB = np.full((128, 3, 4, 3, 128), NEGM, np.float32)
    for g, dil in enumerate((1, 4, 16)):
        for b in range(3):
            off = (b - 1) * 128 + kp - qp
            valid = np.abs(off) <= 64
            bk = _t5_bucket(dil * np.clip(off, -64, 64))
            for j in range(4):
                tabB[:, g, j, b, :] = np.where(valid, rel_bias[bk, 8 + 4 * g + j], NEGM)
    return tabA.reshape(128, -1), tabB.reshape(128, -1)


class Buf:
    __slots__ = ("w", "r")

    def __init__(self):
        self.w = {}
        self.r = {}


class Sched:
    ENGS = ("pe", "act", "dve", "pool", "sp")

    def __init__(self):
        self.q = {e: [] for e in self.ENGS}
        self.cnt = {}
        self.seen = {e: {} for e in self.ENGS}
        self.own = {"pe": "s_pe", "act": "s_act", "dve": "s_dve", "pool": "s_pool"}
        for s in self.own.values():
            self.cnt[s] = 0

    def wait(self, eng, sem, val):
        if self.seen[eng].get(sem, 0) >= val:
            return
        self.seen[eng][sem] = val
        self.q[eng].append(("w", sem, val))

    def run(self, eng, fns, reads=(), writes=(), dsem=None, extra=()):
        if not isinstance(fns, (list, tuple)):
            fns = [fns]
        own = self.own.get(eng)
        deps = {}
        for b in reads:
            for s, v in b.w.items():
                deps[s] = max(deps.get(s, 0), v)
        for b in writes:
            for s, v in list(b.w.items()) + list(b.r.items()):
                deps[s] = max(deps.get(s, 0), v)
        for (s, v) in extra:
            deps[s] = max(deps.get(s, 0), v)
        for s, v in deps.items():
            if eng == "pe" and s == own:
                continue
            self.wait(eng, s, v)
        if dsem is not None:
            assert len(fns) == 1
            self.cnt[dsem] = self.cnt.get(dsem, 0) + 16
            ev = (dsem, self.cnt[dsem])
            self.q[eng].append(("o", fns[0], dsem, 16))
        else:
            for f in fns[:-1]:
                self.q[eng].append(("o", f, None, 0))
            self.cnt[own] += 1
            ev = (own, self.cnt[own])
            self.q[eng].append(("o", fns[-1], own, 1))
        for b in reads:
            b.r[ev[0]] = ev[1]
        for b in writes:
            b.w = {ev[0]: ev[1]}
            b.r = {}
        return ev

    def barrier(self, engs=("pe", "act", "dve", "pool", "sp")):
        snap = dict(self.cnt)
        for e in engs:
            for s, v in snap.items():
                if v > 0 and not s.startswith("d_c"):
                    self.wait(e, s, v)

    def replay(self, eng, e, sems):
        for it in self.q[eng]:
            if it[0] == "w":
                e.wait_ge(sems[it[1]], it[2])
            else:
                inst = it[1](e)
                if it[2] is not None:
                    inst.then_inc(sems[it[2]], it[3])


class _Stop(Exception):
    pass


STAGE = 99


def _ck(n):
    if STAGE == n:
        raise _Stop()


def build(NSEQ):
    nc = bass.Bass("TRN2", target_bir_lowering=False)

    def din(name, shape, dt=F32):
        return nc.dram_tensor(name, shape, dt, kind="ExternalInput").ap()

    x_d = din("x", [NSEQ, S, D])
    mem_d = din("mem", [NSEQ, NMEM, D])
    y_d = nc.dram_tensor("y", [NSEQ, S, D], F32, kind="ExternalOutput").ap()
    w_in_d = din("w_in_p", [D, 6656])
    w_kv_d = din("w_kv", [D, 1024])
    w_a_d = din("w_a", [512, D])
    w_b_d = din("w_b", [256, D])
    w_m_d = din("w_m", [512, D])
    w_o_d = din("w_o", [D, D])
    w_up_d = din("w_up", [D, DFF])
    w_dn_d = din("w_dn", [DFF, D])
    tabA_d = din("tabA", [128, 8 * 3 * 128])
    tabB_d = din("tabB", [128, 3 * 4 * 3 * 128])
    gv_d = din("gv", [128, 24])
    gf_d = din("gf", [1, D])
    sink_d = din("sink", [1, 8])
    ident_d = din("ident", [128, 128])
    wimg = nc.dram_tensor("wimg", [NBLK, 128, SLOT], BF16).ap()

    sc = Sched()
    sem_names = ["s_pe", "s_act", "s_dve", "s_pool", "d_setup", "d_c0", "d_c1", "d_c2"] + \
        ["d_w%d" % i for i in range(NSLOT)] + ["d_x%d" % i for i in range(3)] + ["d_st%d" % i for i in range(4)]

    from contextlib import ExitStack
    with ExitStack() as st:
        def sb(name, shape, dt):
            return st.enter_context(nc.sbuf_tensor("sb_" + name, shape, dt))

        ident_f = sb("ident_f", [128, 128], F32)
        ident = sb("ident", [128, 128], BF16)
        ones_bf = sb("ones_bf", [128, 128], BF16)
        expA = sb("expA", [128, 8, 3, 128], BF16)
        expB = sb("expB", [128, 3, 4, 3, 128], BF16)
        gv = sb("gv", [128, 24], F32)
        gfb = sb("gfb", [128, D], F32)
        sink_sb = sb("sink_sb", [33, 8], F32)
        es_f = sb("es_f", [33, 8], F32)
        es_h32 = sb("es_h32", [33, 8], F32)
        es_hi8 = sb("es_hi8", [33, 8], BF16)
        es_lo8 = sb("es_lo8", [33, 8], BF16)
        es2 = sb("es2", [33, 8, 128], BF16)
        sinkV = sb("sinkV", [33, 128], BF16)
        eps_t = sb("eps_t", [128, 1], F32)
        stat = sb("stat", [128, 4, 4], F32)
        wring = sb("wring", [128, NSLOT, SLOT], BF16)
        xin = sb("xin", [128, 3, D], F32)
        hn = sb("hn", [128, 2, D], BF16)
        sqj = sb("sqj", [128, D], BF16)
        mkT = sb("mkT", [128, 4, NMEM], BF16)
        mv = sb("mv", [128, 2, 512], BF16)
        oA_T = sb("oA_T", [128, 4, S], BF16)
        oB_T = sb("oB_T", [128, 2, S], BF16)
        pbuf = sb("pbuf", [128, 12, 512], BF16)
        rden = sb("rden", [128, 2, 512], F32)
        arena = sb("arena", [128, 88 * 1024 // 2], BF16)
        ps = st.enter_context(nc.psum_tensor("ps", [128, 8, 512], F32))
        sems = {n: st.enter_context(nc.semaphore(n)) for n in sem_names}
        block = st.enter_context(nc.Block())

        def carve(off_bytes, shape, dt):
            n = int(np.prod(shape))
            esz = 2 if dt == BF16 else 4
            a = arena[:, off_bytes // 2: off_bytes // 2 + n * esz // 2]
            if dt != BF16:
                a = a.bitcast(dt)
            if len(shape) == 2:
                pat = "p (a b) -> p a b"
                return a.rearrange(pat, a=shape[0])
            if len(shape) == 3:
                return a.rearrange("p (a b c) -> p a b c", a=shape[0], b=shape[1])
            return a

        KB = 1024
        hT = carve(0, [8, S], BF16)
        qA_T = carve(32 * KB, [4, S], BF16)
        kA_T = carve(48 * KB, [1, S], BF16)
        vA = carve(52 * KB, [16, 2, 128], BF16)
        qB_T = carve(32 * KB, [1, S], BF16)
        kB_T = carve(36 * KB, [1, S], BF16)
        vB = carve(40 * KB, [16, 2, 128], BF16)
        accB = carve(48 * KB, [2, S], F32)
        hTbs = [carve(0, [8, TB], BF16), carve(80 * KB, [8, TB], BF16)]
        memT = carve(64 * KB, [8, NMEM], BF16)
        h2T = carve(8 * KB, [8, TB], BF16)
        x2 = carve(16 * KB, [4, D], F32)
        uT = carve(32 * KB, [32, TB], BF16)
        mergedT = carve(64 * KB, [8, TB], BF16)
        oM_T = carve(72 * KB, [4, TB], BF16)
        qM_T = carve(76 * KB, [4, TB], BF16)

        B_ps = [Buf() for _ in range(8)]
        B_w = [Buf() for _ in range(NSLOT)]
        B_xin = [Buf() for _ in range(3)]
        B_hn = [Buf() for _ in range(2)]
        B_stat = [Buf() for _ in range(4)]
        B_sqj = Buf()
        B_p = [Buf() for _ in range(12)]
        B_rden = [Buf() for _ in range(2)]
        B_memT, B_mkT, B_mv = Buf(), Buf(), Buf()
        B_oA = [Buf() for _ in range(16)]
        B_oB = Buf()
        B_hT = [Buf() for _ in range(16)]
        B_qA = [Buf() for _ in range(4)]
        B_kA = [Buf() for _ in range(4)]
        B_vA = [Buf() for _ in range(4)]
        B_qB, B_kB, B_vB, B_accB = Buf(), Buf(), Buf(), Buf()
        B_hTbs, B_h2T, B_uT, B_mergedT, B_oM, B_qM = [Buf(), Buf()], Buf(), [Buf() for _ in range(32)], [Buf() for _ in range(8)], Buf(), Buf()
        B_x2 = [Buf() for _ in range(4)]
        B_setup = Buf()

        rr = {"ps": 0, "w": 0, "xin": 0, "hn": 0, "stat": 0, "p": 0, "rden": 0}

        def nxt(k, n):
            i = rr[k]
            rr[k] = (i + 1) % n
            return i

        def bank():
            i = nxt("ps", 8)
            return ps[:, i, :], B_ps[i]

        def wload(blk, nelem=4096):
            i = nxt("w", NSLOT)
            csem = "d_c0" if blk in (BLK_A0, BLK_A1) else ("d_c1" if blk <= BLK_M0 else "d_c2")
            sc.run("sp", lambda e, i=i, blk=blk, nelem=nelem: e.dma_start(out=wring[:, i, 0:nelem], in_=wimg[blk, :, 0:nelem]),
                   writes=[B_w[i]], dsem="d_w%d" % i, extra=[(csem, sc.cnt[csem])])
            return wring[:, i, :], B_w[i]

        def wv(slot, off, k, n):
            return slot[:, off:off + k * n].rearrange("p (k n) -> p k n", k=k)

        sc.run("pool", [lambda e: e.memset(eps_t[:], 1e-6),
                        lambda e: e.memset(ones_bf[:], 1.0),
                        lambda e: e.memset(es2[:], 0.0),
                        lambda e: e.memset(sink_sb[:], 0.0),
                        lambda e: e.memset(stat[:], 0.0)], writes=[B_setup])
        B_sinkV = Buf()
        sc.run("pool", lambda e: e.memset(sinkV[:], 0.0), writes=[B_sinkV, B_setup])
        sc.run("pool", lambda e: e.memset(sinkV[0:1, 64:128], 1.0), writes=[B_sinkV, B_setup])
        sc.run("pool", lambda e: e.memset(sinkV[32:33, 64:128], 1.0), writes=[B_sinkV, B_setup])
        pending = {"d_c1": [], "d_c2": []}

        def cast(blk, off, src, k, n, csem, pace=None):
            if csem != "d_c0" and pace is None:
                pending[csem].append((blk, off, src, k, n))
                return
            sc.run("pool", lambda e: e.dma_start(
                out=wimg[blk, :, off:off + k * n].rearrange("p (k n) -> p k n", k=k),
                in_=src.rearrange("(k p) n -> p k n", p=128)), dsem=csem, extra=([pace] if pace else []))

        def issue_cast(csem, pace, n=1):
            for _ in range(n):
                if pending[csem]:
                    cast(*pending[csem].pop(0), csem, pace=pace)

        cast(BLK_A0, 0, w_in_d[:, 0:512], 8, 512, "d_c0")
        cast(BLK_A1, 0, w_in_d[:, 512:768], 8, 256, "d_c0")
        for i in range(6):
            cast(BLK_B + i, 0, w_in_d[:, 768 + 384 * i: 768 + 384 * (i + 1)], 8, 384, "d_c1")
        cast(BLK_KV0, 0, w_kv_d[:, 0:512], 8, 512, "d_c1")
        cast(BLK_KV1, 0, w_kv_d[:, 512:1024], 8, 512, "d_c1")
        cast(BLK_M0, 0, w_in_d[:, 3072:3584], 8, 512, "d_c1")
        for fc in range(8):
            cast(BLK_T1 + fc, 0, w_in_d[:, 3584 + 384 * fc: 3584 + 384 * (fc + 1)], 8, 384, "d_c2")
            cast(BLK_T1 + fc, 3072, w_a_d[:, 128 * fc:128 * (fc + 1)], 4, 128, "d_c2")
            cast(BLK_T1 + fc, 3584, w_b_d[:, 128 * fc:128 * (fc + 1)], 2, 128, "d_c2")
            cast(BLK_T1 + fc, 3840, w_m_d[:, 128 * fc:128 * (fc + 1)], 4, 128, "d_c2")
        for c in range(2):
            cast(BLK_O + c, 0, w_o_d[:, 512 * c:512 * (c + 1)], 8, 512, "d_c2")
        for fq in range(8):
            cast(BLK_U + fq, 0, w_up_d[:, 512 * fq:512 * (fq + 1)], 8, 512, "d_c2")
        for c in range(2):
            for kq in range(4):
                cast(BLK_D + c * 4 + kq, 0, w_dn_d[1024 * kq:1024 * (kq + 1), 512 * c:512 * (c + 1)], 8, 512, "d_c2")

        def sdma(out, in_):
            sc.run("sp", lambda e: e.dma_start(out=out, in_=in_), dsem="d_setup", writes=[B_setup])

        tA = carve(0, [8 * 3 * 128 // 128, 128], F32)
        tB = carve(16 * KB, [3 * 4 * 3 * 128 // 128, 128], F32)
        sdma(ident_f[:], ident_d[:, :])
        sdma(gv[:], gv_d[:, :])
        sdma(gfb[:], gf_d.partition_broadcast(128))
        sdma(sink_sb[0:1, :], sink_d[:, :])
        sdma(sink_sb[32:33, :], sink_d[:, :])
        sdma(tA, tabA_d.rearrange("p (a b) -> p a b", b=128))
        sdma(tB, tabB_d.rearrange("p (a b) -> p a b", b=128))
        sc.run("act", [lambda e: e.activation(out=ident[:], in_=ident_f[:], func=AF.Copy),
                       lambda e: e.activation(out=expA[:].rearrange("p a b c -> p (a b) c"), in_=tA, func=AF.Exp),
                       lambda e: e.activation(out=expB[:].rearrange("p a b c d -> p (a b c) d"), in_=tB, func=AF.Exp),
                       lambda e: e.activation(out=es_f[:], in_=sink_sb[:], func=AF.Exp)],
               reads=[B_setup], writes=[B_setup])
        sc.run("dve", lambda e: e.tensor_copy(out=es_hi8[:], in_=es_f[:]), reads=[B_setup], writes=[B_setup])
        sc.run("dve", lambda e: e.tensor_copy(out=es_h32[:], in_=es_hi8[:]), reads=[B_setup], writes=[B_setup])
        sc.run("dve", lambda e: e.tensor_tensor(out=es_lo8[:], in0=es_f[:], in1=es_h32[:], op=ALU.subtract),
               reads=[B_setup], writes=[B_setup])
        sc.run("dve", [lambda e: e.tensor_copy(out=es2[0:1, :, :], in_=es_hi8[0:1, :].unsqueeze(2).to_broadcast([1, 8, 128])),
                       lambda e: e.tensor_copy(out=es2[32:33, :, :], in_=es_lo8[32:33, :].unsqueeze(2).to_broadcast([1, 8, 128]))],
               reads=[B_setup], writes=[B_setup])
        sc.barrier(("pe", "act", "dve", "pool"))

        def rstd_of(src, src_bufs):
            si = nxt("stat", 4)
            sc.run("act", [lambda e: e.activation(out=sqj[:], in_=src, func=AF.Square, accum_out=stat[:, si, 0:1])],
                   reads=src_bufs, writes=[B_stat[si]])
            sc.run("act", lambda e: e.activation(out=stat[:, si, 1:2], in_=stat[:, si, 0:1], func=AF.Ln,
                                                 scale=1.0 / D, bias=eps_t[:, 0:1]), reads=[B_stat[si]], writes=[B_stat[si]])
            sc.run("act", lambda e: e.activation(out=stat[:, si, 2:3], in_=stat[:, si, 1:2], func=AF.Exp, scale=-0.5),
                   reads=[B_stat[si]], writes=[B_stat[si]])
            return stat[:, si, 2:3], B_stat[si]

        def norm_p1(src, src_bufs, scale_eng="act"):
            r_ap, r_buf = rstd_of(src, src_bufs)
            hi = nxt("hn", 2)
            if scale_eng == "act":
                sc.run("act", lambda e: e.activation(out=hn[:, hi, :], in_=src, func=AF.Copy, scale=r_ap),
                       reads=src_bufs + [r_buf], writes=[B_hn[hi]])
            else:
                sc.run("dve", lambda e: e.tensor_scalar(out=hn[:, hi, :], in0=src, scalar1=r_ap, scalar2=None, op0=ALU.mult),
                       reads=src_bufs + [r_buf], writes=[B_hn[hi]])
            return hi

        def norm_p2(hi, gcol, dsts):
            bk, bb = bank()
            bkb = bk.bitcast(BF16).rearrange("p (k t) -> p k t", k=8)
            sc.run("pe", [(lambda e, k=k: e.transpose(out=bkb[:, k, :], in_=hn[:, hi, 128 * k:128 * (k + 1)], identity=ident[:]))
                          for k in range(8)], reads=[B_hn[hi]], writes=[bb])
            gb = gv[:, gcol:gcol + 8].unsqueeze(2).to_broadcast([128, 8, 128])
            for (o, bufs) in dsts:
                sc.run("dve", lambda e, o=o: e.tensor_tensor(out=o, in0=bkb, in1=gb, op=ALU.mult), reads=[bb], writes=bufs)

        def norm_transpose(src, src_bufs, gcol, dsts):
            norm_p2(norm_p1(src, src_bufs), gcol, dsts)

        def xload(src):
            xi = nxt("xin", 3)
            sc.run("sp", lambda e: e.dma_start(out=xin[:, xi, :], in_=src), writes=[B_xin[xi]], dsem="d_x%d" % xi)
            return xin[:, xi, :], B_xin[xi]

        def proj_fm(slot, sbuf_, woff, wn, ncol0, rhs_fn, rhs_bufs, evac):
            w = wv(slot, woff, 8, wn)
            bk, bb = bank()
            sc.run("pe", [(lambda e, k=k: e.matmul(bk, lhsT=w[:, k, ncol0:ncol0 + 128], rhs=rhs_fn(k),
                                                   start=(k == 0), stop=(k == 7))) for k in range(8)],
                   reads=[sbuf_] + rhs_bufs, writes=[bb])
            evac(bk, bb)

        def copy_evac(eng, out, bufs):
            def f(bk, bb):
                if eng == "act":
                    sc.run("act", lambda e: e.activation(out=out, in_=bk, func=AF.Copy), reads=[bb], writes=bufs)
                else:
                    sc.run("dve", lambda e: e.tensor_copy(out=out, in_=bk), reads=[bb], writes=bufs)
            return f

        def do_seq(sq):
            _ck(0)
            pre_n1 = []
            for t in range(2):
                xa, xb_ = xload(x_d[sq, 128 * t:128 * (t + 1), :])
                pre_n1.append(norm_p1(xa, [xb_]))
            sc.barrier(("act", "dve", "pool"))
            def aproj_groups(tw, slotA0, sA0, slotA1, sA1, wA1):
                hb = B_hT[4 * tw:4 * tw + 4]
                rf = (lambda k: hT[:, k, 512 * tw:512 * (tw + 1)])
                groups = []
                for i in range(4):
                    groups.append(lambda i=i: proj_fm(slotA0, sA0, 0, 512, 128 * i, rf, hb,
                                                      copy_evac("dve", qA_T[:, i, 512 * tw:512 * (tw + 1)], [B_qA[tw]])))
                groups.append(lambda: proj_fm(slotA1, sA1, 0, 256, 0, rf, hb,
                                              copy_evac("dve", kA_T[:, 0, 512 * tw:512 * (tw + 1)], [B_kA[tw]])))

                def vgrp():
                    bk, bb = bank()
                    bk4 = bk.rearrange("p (t g d) -> p t g d", t=4, g=2)
                    fns = []
                    for tt in range(4):
                        t = 4 * tw + tt
                        fns += [(lambda e, k=k, t=t, tt=tt: e.matmul(bk4[:, tt, :, :], lhsT=hT[:, k, 128 * t:128 * (t + 1)],
                                                                     rhs=wA1[:, k, 128:256], start=(k == 0), stop=(k == 7)))
                                for k in range(8)]
                    sc.run("pe", fns, reads=[sA1] + hb, writes=[bb])
                    sc.run("dve", lambda e: e.tensor_copy(out=vA[:, 4 * tw:4 * tw + 4, :, 0:64], in_=bk4), reads=[bb], writes=[B_vA[tw]])
                groups.append(vgrp)
                return groups

            sc.run("pool", lambda e: e.memset(vA[:, :, :, 64:128], 1.0), writes=B_vA)
            pend = []
            for tw in range(4):
                for t in range(4 * tw, 4 * tw + 4):
                    if t < 2:
                        hi_ = pre_n1[t]
                    else:
                        xa, xb_ = xload(x_d[sq, 128 * t:128 * (t + 1), :])
                        hi_ = norm_p1(xa, [xb_], scale_eng=("dve" if t % 2 else "act"))
                    norm_p2(hi_, 0, [(hT[:, :, 128 * t:128 * (t + 1)], [B_hT[t]])])
                    if t >= 3:
                        issue_cast("d_c1", ("s_act", sc.cnt["s_act"]))
                    for _ in range(2):
                        if pend:
                            pend.pop(0)()
                if tw == 3:
                    issue_cast("d_c1", ("s_act", sc.cnt["s_act"]), n=100)
                if tw == 0:
                    slotA0, sA0 = wload(BLK_A0)
                    slotA1, sA1 = wload(BLK_A1, 8 * 256)
                    wA1 = wv(slotA1, 0, 8, 256)
                pend += aproj_groups(tw, slotA0, sA0, slotA1, sA1, wA1)
            while pend:
                pend.pop(0)()
            _ck(1)
            _ck(2)
            for mt in range(2):
                xa, xb_ = xload(mem_d[sq, 128 * mt:128 * (mt + 1), :])
                norm_transpose(xa, [xb_], 16, [(memT[:, :, 128 * mt:128 * (mt + 1)], [B_memT])])
            slot, sbuf_ = wload(BLK_KV0)
            w = wv(slot, 0, 8, 512)
            for hp in range(2):
                bk, bb = bank()
                bk2 = bk.rearrange("p (a b) -> p a b", a=2)
                fns = []
                for hh in range(2):
                    h = 2 * hp + hh
                    fns += [(lambda e, k=k, h=h, hh=hh, bk2=bk2, w=w: e.matmul(bk2[:, hh, :], lhsT=w[:, k, 128 * h:128 * (h + 1)], rhs=memT[:, k, :],
                                                                 start=(k == 0), stop=(k == 7))) for k in range(8)]
                sc.run("pe", fns, reads=[sbuf_, B_memT], writes=[bb])
                sc.run("act", lambda e, hp=hp, bk2=bk2: e.activation(out=mkT[:, 2 * hp:2 * hp + 2, :], in_=bk2, func=AF.Copy),
                       reads=[bb], writes=[B_mkT])
            slot, sbuf_ = wload(BLK_KV1)
            w = wv(slot, 0, 8, 512)
            for mt in range(2):
                bk, bb = bank()
                sc.run("pe", [(lambda e, k=k, mt=mt, bk=bk, w=w: e.matmul(bk, lhsT=memT[:, k, 128 * mt:128 * (mt + 1)], rhs=w[:, k, :],
                                                                          start=(k == 0), stop=(k == 7))) for k in range(8)],
                       reads=[sbuf_, B_memT], writes=[bb])
                sc.run("act", lambda e, mt=mt, bk=bk: e.activation(out=mv[:, mt, :], in_=bk, func=AF.Copy), reads=[bb], writes=[B_mv])

            _ck(3)
            stepsA = list(range(16))

            def A_scores(qb):
                kts = [kt for kt in (qb - 1, qb, qb + 1) if 0 <= kt <= 15]
                banks = {(kt, g): bank() for kt in kts for g in range(2)}
                for kt in kts:
                    for g in range(2):
                        bk, bb = banks[(kt, g)]
                        pl = slice(64 * g, 64 * g + 64)
                        sc.run("pe", lambda e, bk=bk, kt=kt, pl=pl: e.matmul(
                            bk, lhsT=kA_T[pl, 0, 128 * kt:128 * (kt + 1)], rhs=qA_T[pl, :, 128 * qb:128 * (qb + 1)], start=True, stop=True),
                            reads=[B_kA[kt // 4], B_qA[qb // 4]], writes=[bb])
                res = {0: [], 1: []}
                for kt in kts:
                    for g in range(2):
                        bk, bb = banks[(kt, g)]
                        pi = nxt("p", 12)
                        sc.run("act", lambda e, bk=bk, pi=pi: e.activation(out=pbuf[:, pi, :], in_=bk, func=AF.Exp, scale=0.125),
                               reads=[bb], writes=[B_p[pi]])
                        sc.run("dve", lambda e, pi=pi, kt=kt, g=g: e.tensor_tensor(
                            out=pbuf[:, pi, :].rearrange("p (h q) -> p h q", h=4), in0=pbuf[:, pi, :].rearrange("p (h q) -> p h q", h=4),
                            in1=expA[:, 4 * g:4 * g + 4, kt - qb + 1, :], op=ALU.mult), writes=[B_p[pi]])
                        res[g].append((kt, pi))
                return res

            def A_pv(qb, g, res):
                bk, bb = bank()
                fns = []
                for n, (kt, pi) in enumerate(res):
                    fns.append(lambda e, kt=kt, pi=pi, n=n: e.matmul(bk, lhsT=vA[:, kt, g, :], rhs=pbuf[:, pi, :], start=(n == 0), stop=False))
                fns.append(lambda e: e.matmul(bk, lhsT=sinkV[0:33, :], rhs=es2[0:33, 4 * g:4 * g + 4, :], start=False, stop=True))
                kts = [kt for kt, _ in res]
                sc.run("pe", fns, reads=[B_p[pi] for _, pi in res] + [B_vA[kt // 4] for kt in kts], writes=[bb])
                ri = nxt("rden", 2)
                sc.run("act", lambda e: e.activation(out=rden[0:64, ri, :], in_=bk[64:128, :], func=AF.Ln), reads=[bb], writes=[B_rden[ri]])
                sc.run("act", lambda e: e.activation(out=rden[0:64, ri, :], in_=rden[0:64, ri, :], func=AF.Exp, scale=-1.0),
                       reads=[B_rden[ri]], writes=[B_rden[ri]])
                bk4 = bk.rearrange("p (h q) -> p h q", h=4)
                rd4 = rden[:, ri, :].rearrange("p (h q) -> p h q", h=4)
                qs = slice(128 * qb, 128 * (qb + 1))
                sc.run("dve", [lambda e: e.tensor_tensor(out=oA_T[0:64, 2 * g:2 * g + 2, qs], in0=bk4[0:64, 0:4:2, :],
                                                         in1=rd4[0:64, 0:4:2, :], op=ALU.mult),
                               lambda e: e.tensor_tensor(out=oA_T[64:128, 2 * g:2 * g + 2, qs], in0=bk4[0:64, 1:4:2, :],
                                                         in1=rd4[0:64, 1:4:2, :], op=ALU.mult)],
                       reads=[bb, B_rden[ri]], writes=[B_oA[qb]])

            prev = None
            for n in range(len(stepsA) + 1):
                cur = None
                if n < len(stepsA):
                    cur = A_scores(stepsA[n])
                if prev is not None:
                    for g in range(2):
                        A_pv(stepsA[n - 1], g, prev[g])
                        issue_cast("d_c2", ("s_pe", sc.cnt["s_pe"]))
                prev = cur

            def hTb_of(tw):
                return hTbs[(tw + 1) % 2], B_hTbs[(tw + 1) % 2]

            def t_hTb_p1(tw, tt):
                tok0 = 512 * tw
                xa, xb_ = xload(x_d[sq, tok0 + 128 * tt: tok0 + 128 * (tt + 1), :])
                return norm_p1(xa, [xb_], scale_eng=("dve" if tt % 2 else "act"))

            def t_hTb_p2(tw, tt, hi):
                hb, hbuf = hTb_of(tw)
                norm_p2(hi, 0, [(hb[:, :, 128 * tt:128 * (tt + 1)], [hbuf])])

            def t_Mproj(tw):
                hb, hbuf = hTb_of(tw)
                slotM, sM = wload(BLK_M0)
                for h in range(4):
                    proj_fm(slotM, sM, 0, 512, 128 * h, (lambda k: hb[:, k, :]), [hbuf],
                            copy_evac("dve", qM_T[:, h, :], [B_qM]))

            def t_MS(h):
                pis = []
                for mt in range(2):
                    bk, bb = bank()
                    sc.run("pe", lambda e, bk=bk, mt=mt: e.matmul(bk, lhsT=mkT[:, h, 128 * mt:128 * (mt + 1)], rhs=qM_T[:, h, :],
                                                                  start=True, stop=True), reads=[B_mkT, B_qM], writes=[bb])
                    pi = nxt("p", 12)
                    sc.run("act", lambda e, bk=bk, pi=pi: e.activation(out=pbuf[:, pi, :], in_=bk, func=AF.Exp, scale=128 ** -0.5),
                           reads=[bb], writes=[B_p[pi]])
                    pis.append(pi)
                return pis

            def t_MPV(h, pis):
                bkO, bbO = bank()
                bkD, bbD = bank()
                sc.run("pe", [(lambda e, mt=mt: e.matmul(bkO, lhsT=mv[:, mt, 128 * h:128 * (h + 1)], rhs=pbuf[:, pis[mt], :],
                                                         start=(mt == 0), stop=(mt == 1))) for mt in range(2)],
                       reads=[B_p[p] for p in pis] + [B_mv], writes=[bbO])
                sc.run("pe", [(lambda e, mt=mt: e.matmul(bkD, lhsT=ones_bf[:], rhs=pbuf[:, pis[mt], :],
                                                         start=(mt == 0), stop=(mt == 1))) for mt in range(2)],
                       reads=[B_p[p] for p in pis], writes=[bbD])
                ri = nxt("rden", 2)
                sc.run("act", lambda e: e.activation(out=rden[:, ri, :], in_=bkD, func=AF.Ln), reads=[bbD], writes=[B_rden[ri]])
                sc.run("act", lambda e: e.activation(out=rden[:, ri, :], in_=rden[:, ri, :], func=AF.Exp, scale=-1.0),
                       reads=[B_rden[ri]], writes=[B_rden[ri]])
                sc.run("dve", lambda e: e.tensor_tensor(out=oM_T[:, h, :], in0=bkO, in1=rden[:, ri, :], op=ALU.mult),
                       reads=[bbO, B_rden[ri]], writes=[B_oM])

            def t_Mheads():
                prevp = None
                for h in range(5):
                    cur = t_MS(h) if h < 4 else None
                    if prevp is not None:
                        t_MPV(h - 1, prevp)
                    prevp = cur

            def t_T1fc(tw, fc):
                tok0 = 512 * tw
                hb, hbuf = hTb_of(tw)
                sg = pbuf
                slotT, sT = wload(BLK_T1 + fc, SLOT)
                wg = wv(slotT, 0, 8, 384)
                wa = wv(slotT, 3072, 4, 128)
                wb_ = wv(slotT, 3584, 2, 128)
                wm = wv(slotT, 3840, 4, 128)
                gis = []
                for gi in range(3):
                    bk, bb = bank()
                    sc.run("pe", [(lambda e, k=k, bk=bk, gi=gi: e.matmul(bk, lhsT=wg[:, k, 128 * gi:128 * (gi + 1)], rhs=hb[:, k, :],
                                                                        start=(k == 0), stop=(k == 7))) for k in range(8)],
                           reads=[sT, hbuf], writes=[bb])
                    pi = nxt("p", 12)
                    sc.run("act", lambda e, bk=bk, pi=pi: e.activation(out=sg[:, pi, :], in_=bk, func=AF.Sigmoid), reads=[bb], writes=[B_p[pi]])
                    gis.append(pi)
                prods = []
                for bi, (wbr, nk, src, sbufs) in enumerate(((wa, 4, oA_T, B_oA[4 * tw:4 * tw + 4]), (wb_, 2, oB_T, [B_oB]), (wm, 4, oM_T, [B_oM]))):
                    bk, bb = bank()
                    if bi == 2:
                        rf = (lambda k, src=src: src[:, k, :])
                    else:
                        rf = (lambda k, src=src: src[:, k, tok0:tok0 + 512])
                    sc.run("pe", [(lambda e, k=k, bk=bk, wbr=wbr, rf=rf, nk=nk: e.matmul(bk, lhsT=wbr[:, k, :], rhs=rf(k),
                                                                                      start=(k == 0), stop=(k == nk - 1))) for k in range(nk)],
                           reads=[sT] + list(sbufs), writes=[bb])
                    prods.append((bk, bb))
                sc.run("dve", lambda e, b=prods[0][0], p=gis[0]: e.tensor_tensor(out=rden[:, 0, :], in0=b, in1=sg[:, p, :], op=ALU.mult),
                       reads=[prods[0][1], B_p[gis[0]]], writes=[B_rden[0]])
                sc.run("dve", lambda e, b=prods[1][0], p=gis[1]: e.tensor_tensor(out=rden[:, 1, :], in0=b, in1=sg[:, p, :], op=ALU.mult),
                       reads=[prods[1][1], B_p[gis[1]]], writes=[B_rden[1]])
                sc.run("dve", lambda e: e.tensor_tensor(out=rden[:, 0, :], in0=rden[:, 0, :], in1=rden[:, 1, :], op=ALU.add),
                       reads=[B_rden[1]], writes=[B_rden[0]])
                sc.run("dve", lambda e, b=prods[2][0], p=gis[2]: e.tensor_tensor(out=rden[:, 1, :], in0=b, in1=sg[:, p, :], op=ALU.mult),
                       reads=[prods[2][1], B_p[gis[2]]], writes=[B_rden[1]])
                sc.run("dve", lambda e: e.tensor_tensor(out=mergedT[:, fc, :], in0=rden[:, 0, :], in1=rden[:, 1, :], op=ALU.add),
                       reads=[B_rden[0], B_rden[1]], writes=[B_mergedT[fc]])

            def t_T2a(tw, xpre):
                tok0 = 512 * tw
                slots = [wload(BLK_O + c) for c in range(2)]
                wos = [wv(sl, 0, 8, 512) for sl, _ in slots]
                xall = [bank() for _ in range(8)]
                for k in range(8):
                    for c in range(2):
                        for tt in range(4):
                            bk, bb = xall[c * 4 + tt]
                            sc.run("pe", lambda e, k=k, bk=bk, tt=tt, wo=wos[c]: e.matmul(
                                bk, lhsT=mergedT[:, k, 128 * tt:128 * (tt + 1)], rhs=wo[:, k, :], start=(k == 0), stop=(k == 7)),
                                reads=[slots[c][1], B_mergedT[k]], writes=[bb])
                xs = list(xpre)
                for tt in range(4):
                    if tt >= 2:
                        xa, xb_ = xload(x_d[sq, tok0 + 128 * tt: tok0 + 128 * (tt + 1), :])
                    else:
                        xa, xb_ = xs[tt]
                    for c in range(2):
                        bk, bb = xall[c * 4 + tt]
                        sc.run("dve", lambda e, bk=bk, tt=tt, c=c, xa=xa: e.tensor_tensor(out=x2[:, tt, 512 * c:512 * (c + 1)], in0=bk,
                                                                                        in1=xa[:, 512 * c:512 * (c + 1)], op=ALU.add),
                               reads=[bb, xb_], writes=[B_x2[tt]])

            def t_n2_p1(tt):
                return norm_p1(x2[:, tt, :], [B_x2[tt]], scale_eng=("dve" if tt % 2 else "act"))

            def t_n2_p2(tt, hi):
                norm_p2(hi, 8, [(h2T[:, :, 128 * tt:128 * (tt + 1)], [B_h2T])])

            def t_T3(tw):
                tok0 = 512 * tw
                for fq in range(8):
                    slotU, sU = wload(BLK_U + fq)
                    wu = wv(slotU, 0, 8, 512)
                    for fcc in range(4):
                        bk, bb = bank()
                        sc.run("pe", [(lambda e, k=k, bk=bk, fcc=fcc, wu=wu: e.matmul(bk, lhsT=wu[:, k, 128 * fcc:128 * (fcc + 1)], rhs=h2T[:, k, :],
                                                                                   start=(k == 0), stop=(k == 7))) for k in range(8)],
                               reads=[sU, B_h2T], writes=[bb])
                        ch = 4 * fq + fcc
                        sc.run("act", lambda e, bk=bk, ch=ch: e.activation(out=uT[:, ch, :], in_=bk, func=AF.Relu), reads=[bb], writes=[B_uT[ch]])
                    sc.run("dve", lambda e, fq=fq: e.tensor_tensor(out=uT[:, 4 * fq:4 * fq + 4, :], in0=uT[:, 4 * fq:4 * fq + 4, :],
                                                                   in1=uT[:, 4 * fq:4 * fq + 4, :], op=ALU.mult),
                           writes=B_uT[4 * fq:4 * fq + 4])
                for c in range(2):
                    banks = [bank() for _ in range(4)]
                    for kq in range(4):
                        slotD, sD = wload(BLK_D + c * 4 + kq)
                        wd = wv(slotD, 0, 8, 512)
                        for tt in range(4):
                            bk, bb = banks[tt]
                            sc.run("pe", [(lambda e, k=k, bk=bk, tt=tt, wd=wd, kq=kq: e.matmul(
                                bk, lhsT=uT[:, 8 * kq + k, 128 * tt:128 * (tt + 1)], rhs=wd[:, k, :],
                                start=(kq == 0 and k == 0), stop=(kq == 3 and k == 7))) for k in range(8)],
                                reads=[sD] + B_uT[8 * kq:8 * kq + 8], writes=[bb])
                    for tt in range(4):
                        bk, bb = banks[tt]
                        sc.run("dve", lambda e, bk=bk, tt=tt, c=c: e.tensor_tensor(out=x2[:, tt, 512 * c:512 * (c + 1)], in0=bk,
                                                                               in1=x2[:, tt, 512 * c:512 * (c + 1)], op=ALU.add),
                               reads=[bb], writes=[B_x2[tt]])
                for tt in range(4):
                    r_ap, r_buf = rstd_of(x2[:, tt, :], [B_x2[tt]])
                    sc.run("dve", lambda e, tt=tt, r_ap=r_ap: e.scalar_tensor_tensor(out=x2[:, tt, :], in0=x2[:, tt, :], scalar=r_ap, in1=gfb[:],
                                                                                    op0=ALU.mult, op1=ALU.mult),
                           reads=[r_buf], writes=[B_x2[tt]])
                    sc.run("pool", lambda e, tt=tt: e.dma_start(out=y_d[sq, tok0 + 128 * tt: tok0 + 128 * (tt + 1), :], in_=x2[:, tt, :]),
                           reads=[B_x2[tt]], dsem="d_st%d" % tt)

            _ck(4)
            sc.barrier()
            def do_B(jp, g, dil):
                if True:
                    Lg = S // dil
                    nqb = Lg // 128
                    slotB, sB = wload(BLK_B + 2 * g + jp, 8 * 384)
                    wB = wv(slotB, 0, 8, 384)
                    for tw in range(4):
                        hb = B_hT[4 * tw:4 * tw + 4]
                        for which, dst, dbuf in ((0, qB_T, B_qB), (1, kB_T, B_kB)):
                            bk, bb = bank()
                            sc.run("pe", [(lambda e, k=k, bk=bk, which=which, tw=tw: e.matmul(
                                bk, lhsT=wB[:, k, 128 * which:128 * (which + 1)], rhs=hT[:, k, 512 * tw:512 * (tw + 1)],
                                start=(k == 0), stop=(k == 7))) for k in range(8)], reads=[sB] + hb, writes=[bb])
                            if dil == 1:
                                o = dst[:, 0, 512 * tw:512 * (tw + 1)]
                                i_ = bk
                            else:
                                m = 512 // dil
                                o = dst[:, 0, :].rearrange("p (r b) -> p r b", r=dil)[:, :, m * tw:m * (tw + 1)]
                                i_ = bk.rearrange("p (m r) -> p r m", r=dil)
                            eng = "act" if which == 0 else "dve"
                            if eng == "act":
                                sc.run("act", lambda e, o=o, i_=i_: e.activation(out=o, in_=i_, func=AF.Copy), reads=[bb], writes=[dbuf])
                            else:
                                sc.run("dve", lambda e, o=o, i_=i_: e.tensor_copy(out=o, in_=i_), reads=[bb], writes=[dbuf])
                    sc.run("pool", lambda e: e.memset(vB[:, :, :, 64:128], 1.0), writes=[B_vB])
                    for t4 in range(4):
                        bk, bb = bank()
                        bk4 = bk.rearrange("p (t j d) -> p t j d", t=4, j=2)
                        fns = []
                        for tt in range(4):
                            tl = 4 * t4 + tt
                            r, qb = tl // nqb, tl % nqb
                            st0 = dil * 128 * qb + r
                            fns += [(lambda e, k=k, tt=tt, st0=st0, bk4=bk4: e.matmul(
                                bk4[:, tt, :, :], lhsT=hT[:, k, st0:st0 + 127 * dil + 1:dil], rhs=wB[:, k, 256:384],
                                start=(k == 0), stop=(k == 7))) for k in range(8)]
                        sc.run("pe", fns, reads=[sB] + B_hT, writes=[bb])
                        sc.run("dve", lambda e, t4=t4, bk4=bk4: e.tensor_copy(out=vB[:, 4 * t4:4 * t4 + 4, :, 0:64], in_=bk4),
                               reads=[bb], writes=[B_vB])
                    if STAGE == 54:
                        return
                    stepsB = [(r, qb) for r in range(dil) for qb in range(nqb)]

                    def B_scores(r, qb):
                        kts = [kt for kt in (qb - 1, qb, qb + 1) if 0 <= kt < nqb]
                        b0 = kts[0] - qb + 1
                        nb = len(kts)
                        qc = r * Lg + 128 * qb
                        pis = []
                        bks = [bank(), bank()]
                        sc.run("pe", [(lambda e, jj=jj, kt=kt: e.matmul(
                            bks[jj][0][:, 128 * (kt - qb + 1):128 * (kt - qb + 2)],
                            lhsT=kB_T[64 * jj:64 * jj + 64, 0, r * Lg + 128 * kt:r * Lg + 128 * kt + 128],
                            rhs=qB_T[64 * jj:64 * jj + 64, 0, qc:qc + 128], start=True, stop=True)) for kt in kts for jj in range(2)],
                            reads=[B_kB, B_qB], writes=[bks[0][1], bks[1][1]])
                        cs = slice(128 * b0, 128 * (b0 + nb))
                        for jj in range(2):
                            bk, bb = bks[jj]
                            pi = nxt("p", 12)
                            sc.run("act", lambda e, bk=bk, pi=pi: e.activation(out=pbuf[:, pi, cs], in_=bk[:, cs], func=AF.Exp, scale=0.125),
                                   reads=[bb], writes=[B_p[pi]])
                            sc.run("dve", lambda e, pi=pi, jj=jj: e.tensor_tensor(
                                out=pbuf[:, pi, cs].rearrange("p (b q) -> p b q", b=nb),
                                in0=pbuf[:, pi, cs].rearrange("p (b q) -> p b q", b=nb),
                                in1=expB[:, g, 2 * jp + jj, b0:b0 + nb, :], op=ALU.mult), writes=[B_p[pi]])
                            pis.append(pi)
                        return (kts, pis)

                    def B_pv(r, qb, res):
                        kts, pis = res
                        bk, bb = bank()
                        bk2 = bk[:, 0:256].rearrange("p (j q) -> p j q", j=2)
                        fns = []
                        for jj in range(2):
                            for n, kt in enumerate(kts):
                                b = kt - qb + 1
                                fns.append(lambda e, kt=kt, n=n, jj=jj, b=b: e.matmul(
                                    bk2[:, jj, :], lhsT=vB[:, r * nqb + kt, jj, :], rhs=pbuf[:, pis[jj], 128 * b:128 * (b + 1)],
                                    start=(n == 0), stop=(n == len(kts) - 1)))
                        sc.run("pe", fns, reads=[B_p[pi] for pi in pis] + [B_vB], writes=[bb])
                        if dil == 1:
                            sc.run("dve", lambda e: e.tensor_copy(out=accB[:, :, 128 * qb:128 * (qb + 1)], in_=bk2), reads=[bb], writes=[B_accB])
                        else:
                            st0 = dil * 128 * qb + r
                            a = accB[:, :, st0:st0 + 127 * dil + 1:dil]
                            sc.run("dve", lambda e: e.tensor_tensor(out=a, in0=bk2, in1=a, op=ALU.add), reads=[bb], writes=[B_accB])

                    LA = 3
                    resq = []
                    for n in range(len(stepsB) + LA):
                        if n < len(stepsB):
                            resq.append(B_scores(*stepsB[n]))
                        if n >= LA and STAGE not in (55, 56, 57):
                            B_pv(*stepsB[n - LA], resq[n - LA])
                            issue_cast("d_c2", ("s_pe", sc.cnt["s_pe"]))

            def do_Bcomb(jp):
                for w4 in range(4):
                    cs = slice(512 * w4, 512 * (w4 + 1))
                    for jj in range(2):
                        ri = nxt("rden", 2)
                        sc.run("act", lambda e, ri=ri, jj=jj, cs=cs: e.activation(out=rden[0:64, ri, :], in_=accB[64:128, jj, cs], func=AF.Ln),
                               reads=[B_accB], writes=[B_rden[ri]])
                        sc.run("act", lambda e, ri=ri: e.activation(out=rden[0:64, ri, :], in_=rden[0:64, ri, :], func=AF.Exp, scale=-1.0),
                               reads=[B_rden[ri]], writes=[B_rden[ri]])
                        sc.run("dve", lambda e, ri=ri, jj=jj, cs=cs: e.tensor_tensor(
                            out=oB_T[64 * jj:64 * jj + 64, jp, cs], in0=accB[0:64, jj, cs], in1=rden[0:64, ri, :], op=ALU.mult),
                            reads=[B_accB, B_rden[ri]], writes=[B_oB])

            pro = {}
            for jp in range(2):
                for g, dil in enumerate((1, 4, 16)):
                    if (STAGE in (50, 54, 55, 56, 57) and g > 0) or (STAGE == 51 and g > 1) or (STAGE == 53 and g != 2):
                        continue
                    if jp == 1 and STAGE == 99:
                        if g == 0:
                            pro["h"] = [t_hTb_p1(0, 0), t_hTb_p1(0, 1)]
                        elif g == 1:
                            t_hTb_p2(0, 0, pro["h"][0])
                            t_hTb_p2(0, 1, pro["h"][1])
                            pro["h"] = [t_hTb_p1(0, 2), t_hTb_p1(0, 3)]
                        else:
                            t_hTb_p2(0, 2, pro["h"][0])
                            t_hTb_p2(0, 3, pro["h"][1])
                            t_Mproj(0)
                    do_B(jp, g, dil)
                if STAGE in (50, 51, 52, 53, 54, 55, 56, 57):
                    continue
                do_Bcomb(jp)
            issue_cast("d_c2", ("s_pe", sc.cnt["s_pe"]), n=100)
            if STAGE == 99:
                t_Mheads()
            _ck(50)
            _ck(54)
            _ck(56)
            _ck(57)
            _ck(55)
            _ck(51)
            _ck(52)
            _ck(53)

            _ck(5)

            def t_T1(b, hooks_at):
                nb = b + 1 if b < 3 else None
                st = {}
                for fc in range(8):
                    t_T1fc(b, fc)
                    if nb is None:
                        continue
                    if fc == hooks_at[0]:
                        st["h"] = [t_hTb_p1(nb, 0), t_hTb_p1(nb, 1)]
                    elif fc == hooks_at[1]:
                        t_hTb_p2(nb, 0, st["h"][0])
                        t_hTb_p2(nb, 1, st["h"][1])
                        st["h"] = [t_hTb_p1(nb, 2), t_hTb_p1(nb, 3)]
                    elif fc == hooks_at[2]:
                        t_hTb_p2(nb, 2, st["h"][0])
                        t_hTb_p2(nb, 3, st["h"][1])

            t_T1(0, (0, 2, 4))
            for tw in range(4):
                nb = tw + 1 if tw < 3 else None
                if nb is not None:
                    t_Mproj(nb)
                xpre = [xload(x_d[sq, 512 * tw + 128 * tt: 512 * tw + 128 * (tt + 1), :]) for tt in range(2)]
                t_T2a(tw, xpre)
                if nb is not None:
                    t_Mheads()
                h2 = [t_n2_p1(0), t_n2_p1(1)]
                if nb is not None:
                    t_T1fc(nb, 0)
                t_n2_p2(0, h2[0])
                t_n2_p2(1, h2[1])
                h2 = [t_n2_p1(2), t_n2_p1(3)]
                if nb is not None:
                    t_T1fc(nb, 1)
                t_n2_p2(2, h2[0])
                t_n2_p2(3, h2[1])
                if nb is not None:
                    nnb = nb + 1 if nb < 3 else None
                    st = {}
                    for fc in range(2, 8):
                        t_T1fc(nb, fc)
                        if nnb is None:
                            continue
                        if fc == 2:
                            st["h"] = [t_hTb_p1(nnb, 0), t_hTb_p1(nnb, 1)]
                        elif fc == 4:
                            t_hTb_p2(nnb, 0, st["h"][0])
                            t_hTb_p2(nnb, 1, st["h"][1])
                            st["h"] = [t_hTb_p1(nnb, 2), t_hTb_p1(nnb, 3)]
                        elif fc == 6:
                            t_hTb_p2(nnb, 2, st["h"][0])
                            t_hTb_p2(nnb, 3, st["h"][1])
                t_T3(tw)

        try:
            for sq in range(NSEQ):
                do_seq(sq)
        except _Stop:
            pass
        for tt in range(4):
            sc.wait("pool", "d_st%d" % tt, sc.cnt.get("d_st%d" % tt, 0))
            sc.wait("sp", "d_st%d" % tt, sc.cnt.get("d_st%d" % tt, 0))

        @block.tensor
        def _(e):
            sc.replay("pe", e, sems)

        @block.scalar
        def _(e):
            sc.replay("act", e, sems)

        @block.vector
        def _(e):
            sc.replay("dve", e, sems)

        @block.gpsimd
        def _(e):
            sc.replay("pool", e, sems)

        @block.sync
        def _(e):
            sc.replay("sp", e, sems)
    return nc


_NC_CACHE = {}


def _get_nc(nseq):
    if nseq not in _NC_CACHE:
        _NC_CACHE[nseq] = build(nseq)
    return _NC_CACHE[nseq]


def _shared_inputs(rel_bias, norm1_g, w_in, mem_norm_g, w_mem_kv, sink_logit, w_branch_a, w_branch_b,
                   w_branch_m, w_out, norm2_g, w_up, w_down, final_norm_g):
    f = lambda a: np.ascontiguousarray(np.asarray(a, dtype=np.float32))
    tabA, tabB = _bias_tables(f(rel_bias))
    gv = np.concatenate([f(norm1_g)[0].reshape(8, 128).T, f(norm2_g)[0].reshape(8, 128).T,
                         f(mem_norm_g)[0].reshape(8, 128).T], axis=1)
    return {
        "w_in_p": f(f(w_in)[0][:, _w_in_perm()]),
        "w_kv": f(w_mem_kv)[0], "w_a": f(w_branch_a)[0], "w_b": f(w_branch_b)[0], "w_m": f(w_branch_m)[0],
        "w_o": f(w_out)[0], "w_up": f(w_up)[0], "w_dn": f(w_down)[0],
        "tabA": f(tabA), "tabB": f(tabB), "gv": f(gv), "gf": f(final_norm_g).reshape(1, D),
        "sink": f(sink_logit).reshape(1, 8), "ident": np.eye(128, dtype=np.float32),
    }


def kernel(x_prompt, x_sample, mem_prompt, mem_sample, rel_bias, norm1_g, w_in, mem_norm_g, w_mem_kv,
           sink_logit, w_branch_a, w_branch_b, w_branch_m, w_out, norm2_g, w_up, w_down, final_norm_g):
    xp = np.asarray(x_prompt, dtype=np.float32)
    xs = np.asarray(x_sample, dtype=np.float32)
    mp = np.asarray(mem_prompt, dtype=np.float32)
    ms = np.asarray(mem_sample, dtype=np.float32)
    shared = _shared_inputs(rel_bias, norm1_g, w_in, mem_norm_g, w_mem_kv, sink_logit, w_branch_a, w_branch_b,
                            w_branch_m, w_out, norm2_g, w_up, w_down, final_norm_g)
    nseq = 5
    nc = _get_nc(nseq)
    in_maps = []
    for c in range(NCORES):
        m = dict(shared)
        m["x"] = np.ascontiguousarray(np.concatenate([xp[4 * c:4 * c + 4], xs[c:c + 1]], axis=0))
        m["mem"] = np.ascontiguousarray(np.concatenate([mp[4 * c:4 * c + 4], ms[c:c + 1]], axis=0))
        in_maps.append(m)
    res = run_bass_kernel_spmd(nc, in_maps, core_ids=list(range(NCORES)))
    yp = np.empty_like(xp)
    ys = np.empty_like(xs)
    for c in range(NCORES):
        y = res.results[c]["y"]
        yp[4 * c:4 * c + 4] = y[0:4]
        ys[c] = y[4]
    return (yp, ys)
```
